# Optimizing a Trainium2 kernel written in Bass

```python
import math
import jax, jax.numpy as jnp
from jax import lax
import numpy as np

D_MODEL = 1024
BATCH = 16
SEQ = 2048
DEPTH = 1
DEC_BATCH = 16
DEC_SEQ = 4096
PAST_LEN = 128

GRID_W = 64
D_MIX = D_MODEL
CONV_WIDTH = D_MIX // 2
N_HEADS = 8
HEAD_DIM = 64
ATTN_WIDTH = N_HEADS * HEAD_DIM
D_PROJ = 3 * CONV_WIDTH + 3 * ATTN_WIDTH
NA_ROWS = 8
NA_COLS = 16
CONV_K = 3
D_FF = 2816
LN_EPS = 1e-5
ALPHA = (2.0 * DEPTH) ** 0.25
BETA = (8.0 * DEPTH) ** -0.25
NEG_INF = -1e30

kernel_name = "hybrid_shortconv_natten_encoder"


def _layernorm(x, g, b):
    xf = x.astype(jnp.float32)
    mu = jnp.mean(xf, axis=-1, keepdims=True)
    var = jnp.mean(jnp.square(xf - mu), axis=-1, keepdims=True)
    return ((xf - mu) * lax.rsqrt(var + LN_EPS) * g.astype(jnp.float32) + b.astype(jnp.float32)).astype(x.dtype)


def _rmsnorm(x, g):
    xf = x.astype(jnp.float32)
    ms = jnp.mean(jnp.square(xf), axis=-1, keepdims=True)
    return (xf * lax.rsqrt(ms + LN_EPS) * g.astype(jnp.float32)).astype(x.dtype)


def _dwconv3(x, w, b):
    c = x.shape[-1]
    y = lax.conv_general_dilated(
        x, w[:, None, :].astype(x.dtype), window_strides=(1,),
        padding=((CONV_K // 2, CONV_K // 2),),
        dimension_numbers=("NWC", "WIO", "NWC"), feature_group_count=c)
    return y + b.astype(x.dtype)


def _neighbourhood_attention(q, k, v, rpb):
    bsz, t = q.shape[0], q.shape[1]
    rows = t // GRID_W
    kh = min(NA_ROWS, rows)
    scale = HEAD_DIM ** -0.5
    qg = (q * scale).reshape(bsz, rows, GRID_W, N_HEADS, HEAD_DIM)
    kg = k.reshape(bsz, rows, GRID_W, N_HEADS, HEAD_DIM)
    vg = v.reshape(bsz, rows, GRID_W, N_HEADS, HEAD_DIM)

    cols = jnp.arange(GRID_W)
    cstart = jnp.clip(cols - NA_COLS // 2, 0, GRID_W - NA_COLS)
    col_ok = (cols[None, :] >= cstart[:, None]) & (cols[None, :] < cstart[:, None] + NA_COLS)
    dc_idx = jnp.clip(cols[None, :] - cols[:, None], -(NA_COLS - 1), NA_COLS - 1) + NA_COLS - 1
    rpb_f = rpb.astype(jnp.float32)
    bias_full = rpb_f[:, :, dc_idx]
    bias_full = jnp.where(col_ok[None, None], bias_full, NEG_INF)
    bias_full = bias_full.transpose(0, 2, 1, 3)

    def row_block(r):
        rs = jnp.clip(r - kh // 2, 0, rows - kh)
        q_r = lax.dynamic_index_in_dim(qg, r, axis=1, keepdims=False)
        k_r = lax.dynamic_slice_in_dim(kg, rs, kh, axis=1)
        v_r = lax.dynamic_slice_in_dim(vg, rs, kh, axis=1)
        bias = lax.dynamic_slice_in_dim(bias_full, rs - r + NA_ROWS - 1, kh, axis=2)
        s = jnp.einsum("bqhd,brkhd->bhqrk", q_r, k_r).astype(jnp.float32) + bias[None]
        p = jax.nn.softmax(s.reshape(bsz, N_HEADS, GRID_W, kh * GRID_W), axis=-1)
        p = p.reshape(s.shape).astype(v.dtype)
        return jnp.einsum("bhqrk,brkhd->bqhd", p, v_r)

    out = lax.map(row_block, jnp.arange(rows))
    return out.transpose(1, 0, 2, 3, 4).reshape(bsz, t, N_HEADS * HEAD_DIM)


def _layer(x, w_in, conv_w, conv_b, rpb, gn_g, w_o, ln1_g, ln1_b,
           w_up, ffn_conv_w, ffn_conv_b, w_down, ln2_g, ln2_b):
    bsz, t, _ = x.shape
    proj = x @ w_in
    c1, c2, c3 = CONV_WIDTH, 2 * CONV_WIDTH, 3 * CONV_WIDTH
    a1, a2 = c3 + ATTN_WIDTH, c3 + 2 * ATTN_WIDTH
    bg, cg, hc, q, k, v = jnp.split(proj, [c1, c2, c3, a1, a2], axis=-1)
    y_conv = bg * _dwconv3(cg * hc, conv_w, conv_b)
    hs = (bsz, t, N_HEADS, HEAD_DIM)
    y_attn = _neighbourhood_attention(q.reshape(hs), k.reshape(hs), v.reshape(hs), rpb)
    y = jnp.concatenate([_rmsnorm(y_conv, gn_g[:CONV_WIDTH]),
                         _rmsnorm(y_attn, gn_g[CONV_WIDTH:])], axis=-1)
    x = _layernorm(ALPHA * x + y @ w_o, ln1_g, ln1_b)
    u = _dwconv3(x @ w_up, ffn_conv_w, ffn_conv_b)
    gate, val = jnp.split(u, 2, axis=-1)
    x = _layernorm(ALPHA * x + (jax.nn.gelu(gate) * val) @ w_down, ln2_g, ln2_b)
    return x


def _trunk(x, ln_in_g, ln_in_b, w_in, conv_w, conv_b, rpb, gn_g, w_o, ln1_g, ln1_b,
           w_up, ffn_conv_w, ffn_conv_b, w_down, ln2_g, ln2_b):
    x = _layernorm(x, ln_in_g, ln_in_b)
    for i in range(DEPTH):
        x = _layer(x, w_in[i], conv_w[i], conv_b[i], rpb[i], gn_g[i], w_o[i], ln1_g[i], ln1_b[i],
                   w_up[i], ffn_conv_w[i], ffn_conv_b[i], w_down[i], ln2_g[i], ln2_b[i])
    return x


def setup_inputs(seed: int = 0) -> dict:
    key = jax.random.key(seed)
    ks = jax.random.split(key, 20)
    f32 = jnp.float32
    nrm = lambda k, shape, s: jax.random.normal(k, shape, f32) * s
    return {
        "x_prompt": nrm(ks[0], (BATCH, SEQ, D_MODEL), 1.0),
        "x_sample": nrm(ks[1], (DEC_BATCH, DEC_SEQ, D_MODEL), 1.0),
        "ln_in_g": 1.0 + nrm(ks[2], (D_MODEL,), 0.02),
        "ln_in_b": nrm(ks[3], (D_MODEL,), 0.02),
        "w_in": nrm(ks[4], (DEPTH, D_MODEL, D_PROJ), D_MODEL ** -0.5),
        "conv_w": nrm(ks[5], (DEPTH, CONV_K, CONV_WIDTH), CONV_K ** -0.5),
        "conv_b": nrm(ks[6], (DEPTH, CONV_WIDTH), 0.02),
        "rpb": nrm(ks[7], (DEPTH, N_HEADS, 2 * NA_ROWS - 1, 2 * NA_COLS - 1), 0.1),
        "gn_g": 1.0 + nrm(ks[8], (DEPTH, D_MIX), 0.02),
        "w_o": nrm(ks[9], (DEPTH, D_MIX, D_MODEL), BETA * D_MIX ** -0.5),
        "ln1_g": 1.0 + nrm(ks[10], (DEPTH, D_MODEL), 0.02),
        "ln1_b": nrm(ks[11], (DEPTH, D_MODEL), 0.02),
        "w_up": nrm(ks[12], (DEPTH, D_MODEL, 2 * D_FF), D_MODEL ** -0.5),
        "ffn_conv_w": nrm(ks[13], (DEPTH, CONV_K, 2 * D_FF), CONV_K ** -0.5),
        "ffn_conv_b": nrm(ks[14], (DEPTH, 2 * D_FF), 0.02),
        "w_down": nrm(ks[15], (DEPTH, D_FF, D_MODEL), BETA * D_FF ** -0.5),
        "ln2_g": 1.0 + nrm(ks[16], (DEPTH, D_MODEL), 0.02),
        "ln2_b": nrm(ks[17], (DEPTH, D_MODEL), 0.02),
    }


def reference(x_prompt, x_sample, ln_in_g, ln_in_b, w_in, conv_w, conv_b, rpb, gn_g, w_o,
              ln1_g, ln1_b, w_up, ffn_conv_w, ffn_conv_b, w_down, ln2_g, ln2_b):
    y_prompt = _trunk(x_prompt, ln_in_g, ln_in_b, w_in, conv_w, conv_b, rpb, gn_g, w_o,
                      ln1_g, ln1_b, w_up, ffn_conv_w, ffn_conv_b, w_down, ln2_g, ln2_b)
    y_sample = _trunk(x_sample, ln_in_g, ln_in_b, w_in, conv_w, conv_b, rpb, gn_g, w_o,
                      ln1_g, ln1_b, w_up, ffn_conv_w, ffn_conv_b, w_down, ln2_g, ln2_b)
    return (y_prompt, y_sample)
```

```python
import numpy as np
from contextlib import ExitStack
import concourse.bass as bass
import concourse.mybir as mybir
from concourse.bass_utils import run_bass_kernel_spmd

F32 = mybir.dt.float32
BF16 = mybir.dt.bfloat16
AF = mybir.ActivationFunctionType
ALU = mybir.AluOpType

D = 1024
DP = 3072
DFF = 2816
NFC = 22
GRID_W = 64
ALPHA = float(2.0 ** 0.25)
EPS = 1e-5
NEG = -1e30
ENGS = ("pe", "act", "dve", "pool", "sp")
GELU_FUNC = "Gelu_apprx_tanh"
BIAS_ON_PE = False
S_LOOKAHEAD = 3
P2_START = 1
LN1_AFF = "pool"


class Op:
    __slots__ = ("eng", "fn", "waits", "sig", "is_dma", "chan", "idx", "sigval")

    def __init__(self, eng, fn, is_dma=False, chan=None):
        self.eng = eng
        self.fn = fn
        self.waits = {}
        self.sig = False
        self.is_dma = is_dma
        self.chan = chan
        self.sigval = None


class Prog:
    def __init__(self, nc):
        self.nc = nc
        self.ops = {e: [] for e in ENGS}
        self.last_w = {}
        self.readers = {}
        self.chan_last = {}
        self.chan_cnt = {}
        self.final_dmas = []

    def _dep(self, op, d):
        if d is op:
            return
        if d.is_dma:
            key = ("c", d.chan)
            val = d.sigval
            cur = op.waits.get(key)
            if cur is None or val > cur:
                op.waits[key] = val
            return
        if d.eng == op.eng and not op.is_dma and d.eng == "pe":
            return
        d.sig = True
        key = ("e", d.eng)
        cur = op.waits.get(key)
        if cur is None or d.idx > cur.idx:
            op.waits[key] = d

    def add(self, eng, fn, reads=(), writes=(), is_dma=False, chan=None):
        op = Op(eng, fn, is_dma, chan)
        op.idx = len(self.ops[eng])
        if is_dma:
            n = self.chan_cnt.get(chan, 0) + 1
            self.chan_cnt[chan] = n
            op.sigval = 16 * n
            prev = self.chan_last.get(chan)
            if prev is not None:
                self._dep(op, prev)
            self.chan_last[chan] = op
        for r in reads:
            w = self.last_w.get(r)
            if w is not None:
                self._dep(op, w)
        for r in writes:
            w = self.last_w.get(r)
            if w is not None and (w.is_dma or op.is_dma or w.eng != op.eng):
                self._dep(op, w)
            for rd in self.readers.get(r, ()):
                if rd.is_dma or op.is_dma or rd.eng != op.eng:
                    self._dep(op, rd)
        for r in reads:
            self.readers.setdefault(r, []).append(op)
        for r in writes:
            self.last_w[r] = op
            self.readers[r] = []
        self.ops[eng].append(op)
        return op

    def dma(self, chan, out, in_, reads=(), writes=(), final=False, eng="sp"):
        op = self.add(eng, lambda e: e.dma_start(out=out, in_=in_), reads, writes, is_dma=True, chan=chan)
        if final:
            self.final_dmas.append(op)
        return op

    def barrier(self):
        lasts = []
        for e in ENGS:
            for op in reversed(self.ops[e]):
                if not op.is_dma and op.fn is not None:
                    lasts.append(op)
                    break
        lasts += list(self.chan_last.values())
        for e in ENGS:
            op = Op(e, None)
            op.idx = len(self.ops[e])
            for d in lasts:
                if d.is_dma:
                    self._dep(op, d)
                elif d.eng != e:
                    self._dep(op, d)
                elif e not in ("pe",):
                    self._dep(op, d)
            self.ops[e].append(op)
        self.last_w = {}
        self.readers = {}

    def emit(self, stack):
        nc = self.nc
        for e in ENGS:
            n = 0
            for op in self.ops[e]:
                if not op.is_dma and op.sig:
                    n += 1
                    op.sigval = n
        esem = {e: stack.enter_context(nc.semaphore("s_" + e)) for e in ENGS if e != "sp"}
        csem = {}
        for i, ch in enumerate(self.chan_cnt):
            csem[ch] = stack.enter_context(nc.semaphore("c%d" % i))
        self.nsem = len(esem) + len(csem)
        block = stack.enter_context(nc.Block())
        finals = self.final_dmas

        def run(engname):
            def body(eng):
                waited = {}
                for op in self.ops[engname]:
                    for key, val in op.waits.items():
                        if key[0] == "c":
                            sem = csem[key[1]]
                            v = val
                        else:
                            sem = esem[key[1]]
                            v = val.sigval
                        if waited.get(key, 0) >= v:
                            continue
                        waited[key] = v
                        eng.wait_ge(sem, v)
                    if op.fn is None:
                        continue
                    ins = op.fn(eng)
                    if op.is_dma:
                        ins.then_inc(csem[op.chan], 16)
                    elif op.sig:
                        ins.then_inc(esem[engname], 1)
                if engname == "sp":
                    for ch, cnt in self.chan_cnt.items():
                        eng.wait_ge(csem[ch], 16 * cnt)
            return body

        block.tensor(run("pe"))
        block.scalar(run("act"))
        block.vector(run("dve"))
        block.gpsimd(run("pool"))
        block.sync(run("sp"))


class Rot:
    def __init__(self, name, bufs, chan0=None):
        self.name = name
        self.b = bufs
        self.i = 0
        self.chan0 = chan0

    def next(self):
        k = self.i % len(self.b)
        self.i += 1
        ch = None if self.chan0 is None else ("ch", self.chan0 + k)
        return self.b[k], (self.name, k), ch


class Arena:
    def __init__(self, nc, stack, nbytes):
        self.t = stack.enter_context(nc.sbuf_tensor("arena", [128, nbytes // 4], F32))
        self.cap = nbytes
        self.off = 0

    def _alloc(self, nbytes):
        nbytes = (nbytes + 63) // 64 * 64
        o = self.off
        self.off += nbytes
        assert self.off <= self.cap, ("SBUF arena overflow", self.off, self.cap)
        return o

    def f32(self, shape):
        n = int(np.prod(shape))
        o = self._alloc(n * 4)
        v = self.t[:, o // 4:o // 4 + n]
        return self._shape(v, shape)

    def bf(self, shape):
        n = int(np.prod(shape))
        assert n % 2 == 0
        o = self._alloc(n * 2)
        v = self.t[:, o // 4:o // 4 + n // 2].bitcast(BF16)
        return self._shape(v, shape)

    @staticmethod
    def _shape(v, shape):
        if len(shape) == 1:
            return v
        if len(shape) == 2:
            return v.rearrange("p (a b) -> p a b", a=shape[0])
        if len(shape) == 3:
            return v.rearrange("p (a b c) -> p a b c", a=shape[0], b=shape[1])
        raise ValueError(shape)


VARIANTS = ("int", "top0", "top2", "botA", "botB")
VAR_NCH = {"int": 5, "top0": 4, "top2": 4, "botA": 4, "botB": 4}
VAR_BASE = {"int": 1, "top0": 6, "top2": 10, "botA": 14, "botB": 18}
NBLK = 22


def rs_of(r, R):
    return min(max(r - 4, 0), R - 8)


def tt_geom(R, r0):
    lo = rs_of(r0, R)
    hi = rs_of(r0 + 1, R) + 8
    nrows = hi - lo
    nch = (nrows + 1) // 2
    if r0 == 0:
        v = "top0"
    elif r0 == 2:
        v = "top2"
    elif r0 == R - 4:
        v = "botA"
    elif r0 == R - 2:
        v = "botB"
    else:
        v = "int"
    if R == 8:
        v = {0: "top0", 2: "top2", 4: "botA", 6: "botB"}[r0]
    assert VAR_NCH[v] == nch, (R, r0, v, nch)
    return v, lo, nch


def bias_index_tables():
    R = 32
    r0_of = {"int": 8, "top0": 0, "top2": 2, "botA": R - 4, "botB": R - 2}
    dr_idx = np.zeros((NBLK, 128, 128), np.int64)
    dc_idx = np.zeros((NBLK, 128, 128), np.int64)
    valid = np.zeros((NBLK, 128, 128), bool)
    cols = np.arange(GRID_W)
    cstart = np.clip(cols - 8, 0, GRID_W - 16)
    for v in VARIANTS:
        r0 = r0_of[v]
        var, lo, nch = tt_geom(R, r0)
        assert var == v
        for c in range(nch):
            b = VAR_BASE[v] + c
            dup = (v == "int" and c == 4)
            for jr in range(2):
                for qr in range(2):
                    j = lo + 2 * c + jr
                    r = r0 + qr
                    rsr = rs_of(r, R)
                    rowok = (rsr <= j < rsr + 8)
                    kc = cols[:, None]
                    qc = cols[None, :]
                    colok = (kc >= cstart[None, :]) & (kc < cstart[None, :] + 16)
                    dri = np.clip(j - r + 7, 0, 14)
                    dci = np.clip(kc - qc, -15, 15) + 15
                    sl = (b, slice(jr * 64, jr * 64 + 64), slice(qr * 64, qr * 64 + 64))
                    dr_idx[sl] = dri
                    dc_idx[sl] = dci
                    valid[sl] = colok & rowok
                    if dup:
                        sl0 = (0,) + sl[1:]
                        dr_idx[sl0] = dri
                        dc_idx[sl0] = dci
                        valid[sl0] = colok & rowok
    return dr_idx, dc_idx, valid


class Builder:
    def __init__(self, seq_lens, debug=False, stages="ABC"):
        self.seq_lens = list(seq_lens)
        assert all(s % 512 == 0 and s >= 512 for s in self.seq_lens)
        self.ntok = sum(self.seq_lens)
        self.debug = debug
        self.stages = stages
        self.tt_info = []
        base = 0
        for s in self.seq_lens:
            R = s // 64
            for i in range(s // 128):
                self.tt_info.append((base, R, 2 * i))
            base += s
        self.nmacro = self.ntok // 512

    def chan(self):
        self._chan += 1
        return self._chan - 1

    def mk_rot(self, name, bufs, dma=True):
        c0 = None
        if dma:
            c0 = self._chan
            self._chan += len(bufs)
        return Rot(name, bufs, c0)

    def layernorm(self, z, zk, out, outk, gbc, bbc, gk, bk, mode="pow"):
        for _ in self.layernorm_gen(z, zk, out, outk, gbc, bbc, gk, bk, mode):
            pass

    def layernorm_gen(self, z, zk, out, outk, gbc, bbc, gk, bk, mode="pow", aff="pool"):
        P = self.P
        st, stk, _ = self.st_rot.next()
        P.add("dve", lambda e: e.bn_stats(out=st[:, 0:6], in_=z[:, 0:512]), reads=[zk], writes=[stk])
        P.add("dve", lambda e: e.bn_stats(out=st[:, 6:12], in_=z[:, 512:1024]), reads=[zk], writes=[stk])
        P.add("dve", lambda e: e.bn_aggr(out=st[:, 12:14], in_=st[:, 0:12]), reads=[stk], writes=[stk])
        yield
        if mode == "pow":
            P.add("dve", lambda e: e.tensor_scalar(out=st[:, 14:15], in0=st[:, 13:14], scalar1=EPS, scalar2=None, op0=ALU.add),
                  reads=[stk], writes=[stk])
            P.add("pool", lambda e: e.tensor_tensor(out=st[:, 15:16], in0=st[:, 14:15], in1=self.nh[:, 0:1], op=ALU.pow),
                  reads=[stk], writes=[stk])
        elif mode == "sqrt":
            P.add("act", lambda e: e.activation(out=st[:, 14:15], in_=st[:, 13:14], func=AF.Sqrt, bias=self.epsc[:, 0:1]),
                  reads=[stk], writes=[stk])
            P.add("dve", lambda e: e.reciprocal(out=st[:, 15:16], in_=st[:, 14:15]), reads=[stk], writes=[stk])
        else:
            P.add("act", lambda e: e.activation(out=st[:, 14:15], in_=st[:, 13:14], func=AF.Ln, bias=self.epsc[:, 0:1]),
                  reads=[stk], writes=[stk])
            P.add("act", lambda e: e.activation(out=st[:, 15:16], in_=st[:, 14:15], func=AF.Exp, scale=-0.5),
                  reads=[stk], writes=[stk])
        yield
        P.add("dve", lambda e: e.tensor_scalar(out=st[:, 16:17], in0=st[:, 12:13], scalar1=st[:, 15:16], scalar2=-1.0,
                                               op0=ALU.mult, op1=ALU.mult), reads=[stk], writes=[stk])
        if mode == "lnexp":
            P.add("dve", lambda e: e.tensor_scalar(out=out, in0=z, scalar1=st[:, 15:16], scalar2=st[:, 16:17], op0=ALU.mult, op1=ALU.add),
                  reads=[zk, stk], writes=[outk])
        else:
            P.add("act", lambda e: e.activation(out=out, in_=z, func=AF.Identity, scale=st[:, 15:16], bias=st[:, 16:17]),
                  reads=[zk, stk], writes=[outk])
        yield
        P.add(aff, lambda e: e.tensor_tensor(out=out, in0=out, in1=gbc, op=ALU.mult), reads=[outk, gk], writes=[outk])
        P.add(aff, lambda e: e.tensor_tensor(out=out, in0=out, in1=bbc, op=ALU.add), reads=[outk, bk], writes=[outk])

    def transposes_f32(self, src, srck, pp, bank0, dst_fn, dstk):
        P = self.P
        for c in range(8):
            half = c // 4
            outap = pp[:, half * 512 + (c % 4) * 128: half * 512 + (c % 4 + 1) * 128]
            P.add("pe", lambda e, outap=outap, c=c: e.transpose(out=outap, in_=src[:, c * 128:(c + 1) * 128], identity=self.idf),
                  reads=[srck], writes=[("ps", bank0 + half)])
        for half in range(2):
            src_ps = pp[:, half * 512:(half + 1) * 512].rearrange("p (a b) -> p a b", a=4)
            dst = dst_fn(half)
            if half == 0:
                P.add("act", lambda e, dst=dst, src_ps=src_ps: e.activation(out=dst, in_=src_ps, func=AF.Copy),
                      reads=[("ps", bank0 + half)], writes=[dstk])
            else:
                P.add("dve", lambda e, dst=dst, src_ps=src_ps: e.tensor_copy(out=dst, in_=src_ps),
                      reads=[("ps", bank0 + half)], writes=[dstk])

    def build(self):
        nc = bass.Bass("TRN2", target_bir_lowering=False)
        self.nc = nc
        nt = self.ntok
        dbg = self.debug
        inp = lambda name, shape, dt=F32: nc.dram_tensor(name, list(shape), dt, kind="ExternalInput").ap()
        scratch_kind = "ExternalOutput" if dbg else "Internal"
        scr = lambda name, shape, dt: nc.dram_tensor(name, list(shape), dt, kind=scratch_kind).ap()
        self.x = inp("x", [nt, D])
        self.w_in = inp("w_in", [128, 8, DP])
        self.w_o = inp("w_o", [128, 8, D])
        self.w_up = inp("w_up", [128, 8, 2 * DFF])
        self.w_dn = inp("w_dn", [128, NFC, D])
        self.ln_in_g = inp("ln_in_g", [D]); self.ln_in_b = inp("ln_in_b", [D])
        self.ln1_g = inp("ln1_g", [D]); self.ln1_b = inp("ln1_b", [D])
        self.ln2_g = inp("ln2_g", [D]); self.ln2_b = inp("ln2_b", [D])
        self.gn2 = inp("gn2", [512])
        self.smallp = inp("smallp", [128, 256])
        self.ident = inp("ident", [128, 128])
        self.rpbg = inp("rpbg", [128, 8, NBLK * 128])
        self.amask = inp("amask", [128, NBLK * 128])
        self.y = nc.dram_tensor("y", [nt, D], F32, kind="ExternalOutput").ap()
        self.d_xh = scr("d_xh", [nt, D], F32)
        self.d_qT = scr("d_qT", [4, 128, nt], BF16)
        self.d_kT = scr("d_kT", [4, 128, nt], BF16)
        self.d_va = scr("d_va", [nt, 520], BF16)
        self.d_bgT = scr("d_bgT", [4, 128, nt], F32)
        self.d_chT = scr("d_chT", [4, 128, nt], F32)
        self.d_x1 = scr("d_x1", [nt, D], F32)
        self.d_x1T = scr("d_x1T", [8, 128, nt], BF16)
        self.d_wup = nc.dram_tensor("d_wup", [128, 8 * 2 * DFF], BF16, kind="Internal").ap()
        self.d_wdn = nc.dram_tensor("d_wdn", [128, NFC * D], BF16, kind="Internal").ap()
        self.d_wo = nc.dram_tensor("d_wo", [128, 8 * D], BF16, kind="Internal").ap()
        self.d_bias = nc.dram_tensor("d_bias", [128, 8 * NBLK * 128], BF16, kind="Internal").ap()
        if dbg:
            self.d_yT = scr("d_yT", [8, 128, nt], BF16)

        with ExitStack() as st:
            self.A = Arena(nc, st, 207 * 1024 + 512)
            self.ps = st.enter_context(nc.psum_tensor("ps", [128, 4096], F32))
            self.pp = [self.ps[:, 1024 * i:1024 * (i + 1)] for i in range(4)]
            self.P = Prog(nc)
            self._chan = 0
            self.setup_consts()
            if "A" in self.stages:
                self.stage_A()
                self.P.barrier()
            if "B" in self.stages:
                self.stage_B()
                self.P.barrier()
            if "C" in self.stages:
                self.stage_C()
            self.P.emit(st)
        return nc

    def setup_consts(self):
        P, A = self.P, self.A
        self.idf = A.f32([128])
        self.idb = A.bf([128])
        self.nh = A.f32([512])
        self.sp_t = A.f32([256])
        self.onesf = A.f32([128])
        self.epsc = A.f32([16])
        self.st_rot = Rot("st", [A.f32([32]) for _ in range(6)])
        P.dma(("ch", self.chan()), self.idf, self.ident, writes=["idf"])
        P.dma(("ch", self.chan()), self.sp_t, self.smallp, writes=["smallp"])
        P.add("dve", lambda e: e.tensor_copy(out=self.idb, in_=self.idf), reads=["idf"], writes=["idb"])
        P.add("pool", lambda e: e.memset(self.nh, -0.5), writes=["nh"])
        P.add("pool", lambda e: e.memset(self.onesf, 1.0), writes=["onesf"])
        P.add("pool", lambda e: e.memset(self.epsc, EPS), writes=["epsc"])
        sp = self.sp_t
        self.cw = sp[:, 0:12].rearrange("p (c k) -> p c k", c=4)
        self.cb = sp[:, 12:16]
        self.gn1 = sp[:, 16:20]
        self.fw = sp[:, 20:152].rearrange("p (c k) -> p c k", c=44)
        self.fb = sp[:, 152:196]
        self.persist_end = A.off
        self.P.barrier()

    def stage_A(self):
        P, A, pp = self.P, self.A, self.pp
        A.off = self.persist_end
        self._chan = 4
        win = A.bf([8, DP])
        gbc = A.f32([D]); bbc = A.f32([D])
        mark = A.off
        stg = [A.f32([DP]) for _ in range(2)]
        P.dma(("ch", self.chan()), gbc, self.ln_in_g.partition_broadcast(128), writes=["gbc"])
        P.dma(("ch", self.chan()), bbc, self.ln_in_b.partition_broadcast(128), writes=["bbc"])
        c0 = self._chan; self._chan += 2
        for kc in range(8):
            s = stg[kc % 2]; sk = ("wstg", kc % 2)
            P.dma(("ch", c0 + kc % 2), s, self.w_in[:, kc, :], writes=[sk])
            P.add("dve", lambda e, s=s, kc=kc: e.tensor_copy(out=win[:, kc, 0:1536], in_=s[:, 0:1536]), reads=[sk], writes=[("win", kc, 0)])
            P.add("act", lambda e, s=s, kc=kc: e.mul(out=win[:, kc, 1536:2048], in_=s[:, 1536:2048], mul=0.125), reads=[sk], writes=[("win", kc, 1)])
            P.add("pool", lambda e, s=s, kc=kc: e.tensor_copy(out=win[:, kc, 2048:3072], in_=s[:, 2048:3072]), reads=[sk], writes=[("win", kc, 2)])
        P.barrier()
        A.off = mark
        xin_rot = self.mk_rot("xin", [A.f32([D]) for _ in range(4)])
        xh_rot = self.mk_rot("xh", [A.f32([D]) for _ in range(2)])
        xT_rot = self.mk_rot("xT", [A.bf([8, 512]) for _ in range(2)], dma=False)
        e32 = self.mk_rot("e32", [A.f32([512]) for _ in range(6)])
        e16 = self.mk_rot("e16", [A.bf([512]) for _ in range(4)])
        cg_rot = self.mk_rot("cg", [A.f32([512]) for _ in range(2)], dma=False)
        va_rot = self.mk_rot("va", [A.bf([8, 65]) for _ in range(3)])
        for i, b in enumerate(va_rot.b):
            P.add("pool", lambda e, b=b: e.memset(b, 1.0), writes=[("va", i)])
        ntt = len(self.tt_info)
        xins = {}

        def load_x(g):
            b, k, ch = xin_rot.next()
            P.dma(ch, b, self.x[g * 128:(g + 1) * 128, :], writes=[k])
            xins[g] = (b, k)

        xTs = {}
        state = dict(bank_rr=0, evac_rr=0)

        def LN_T(m):
            xT, xTk, _ = xT_rot.next()
            xTs[m] = (xT, xTk)
            for j in range(4):
                g = 4 * m + j
                xin, xk = xins[g]
                xh, xhk, xhch = xh_rot.next()
                for _ in self.layernorm_gen(xin, xk, xh, xhk, gbc, bbc, "gbc", "bbc", mode="sqrt"):
                    yield
                P.dma(xhch, self.d_xh[g * 128:(g + 1) * 128, :], xh, reads=[xhk])
                yield
                yield
                ppt = pp[j % 2]
                self.transposes_f32(xh, xhk, ppt, 2 * (j % 2),
                                    lambda half, xT=xT, j=j: xT[:, 4 * half:4 * half + 4, 128 * j:128 * (j + 1)], xTk)
                yield

        def PROJ(m):
            t0 = 512 * m
            xT, xTk = xTs[m]
            order = [0, 1, 2, 3, 4, 8, 5, 9, 6, 10, 7, 11, 12, 13, 14, 15, 16, 17, 18, 19]
            cgbuf = None
            for oc in order:
                bank = 4 + state["bank_rr"] % 4
                state["bank_rr"] += 1
                pso = pp[bank // 2][:, (bank % 2) * 512:(bank % 2 + 1) * 512]
                for kc in range(8):
                    P.add("pe", lambda e, pso=pso, kc=kc, oc=oc: e.matmul(pso, lhsT=win[:, kc, oc * 128:(oc + 1) * 128], rhs=xT[:, kc, :],
                                                                  start=(kc == 0), stop=(kc == 7)),
                          reads=[xTk], writes=[("ps", bank)])
                if oc < 4:
                    b, k, ch = e32.next()
                    P.add("act", lambda e, b=b, pso=pso: e.activation(out=b, in_=pso, func=AF.Copy), reads=[("ps", bank)], writes=[k])
                    P.dma(ch, self.d_bgT[oc, :, t0:t0 + 512], b, reads=[k])
                elif oc < 8:
                    b, k, _ = cg_rot.next()
                    P.add("act", lambda e, b=b, pso=pso: e.activation(out=b, in_=pso, func=AF.Copy), reads=[("ps", bank)], writes=[k])
                    cgbuf = (b, k)
                elif oc < 12:
                    b, k, ch = e32.next()
                    cb_, ck_ = cgbuf
                    P.add("dve", lambda e, b=b, pso=pso, cb_=cb_: e.tensor_tensor(out=b, in0=pso, in1=cb_, op=ALU.mult),
                          reads=[("ps", bank), ck_], writes=[k])
                    P.dma(ch, self.d_chT[oc - 8, :, t0:t0 + 512], b, reads=[k])
                else:
                    b, k, ch = e16.next()
                    if state["evac_rr"] % 2 == 0:
                        P.add("act", lambda e, b=b, pso=pso: e.activation(out=b, in_=pso, func=AF.Copy), reads=[("ps", bank)], writes=[k])
                    else:
                        P.add("dve", lambda e, b=b, pso=pso: e.tensor_copy(out=b, in_=pso), reads=[("ps", bank)], writes=[k])
                    state["evac_rr"] += 1
                    dst = self.d_qT if oc < 16 else self.d_kT
                    P.dma(ch, dst[(oc - 12) % 4, :, t0:t0 + 512], b, reads=[k])
                yield
            for j in range(4):
                g = 4 * m + j
                bank = 4 + state["bank_rr"] % 4
                state["bank_rr"] += 1
                pso = pp[bank // 2][:, (bank % 2) * 512:(bank % 2 + 1) * 512]
                for kc in range(8):
                    P.add("pe", lambda e, pso=pso, kc=kc, j=j: e.matmul(pso, lhsT=xT[:, kc, 128 * j:128 * (j + 1)], rhs=win[:, kc, 2560:3072],
                                                                start=(kc == 0), stop=(kc == 7)),
                          reads=[xTk], writes=[("ps", bank)])
                b, k, ch = va_rot.next()
                src = pso.rearrange("p (h d) -> p h d", h=8)
                if state["evac_rr"] % 2 == 0:
                    P.add("act", lambda e, b=b, src=src: e.activation(out=b[:, :, 0:64], in_=src, func=AF.Copy), reads=[("ps", bank)], writes=[k])
                else:
                    P.add("dve", lambda e, b=b, src=src: e.tensor_copy(out=b[:, :, 0:64], in_=src), reads=[("ps", bank)], writes=[k])
                state["evac_rr"] += 1
                P.dma(ch, self.d_va[g * 128:(g + 1) * 128, :], b.rearrange("p h d -> p (h d)"), reads=[k])
                yield

        PIECE = 1408
        wc_in = self.mk_rot("wcin", [A.f32([PIECE]) for _ in range(2)])
        wc_out = self.mk_rot("wcout", [A.bf([PIECE]) for _ in range(2)])

        def WC():
            srcs = [(self.w_up.rearrange("p a b -> p (a b)"), self.d_wup, 8 * 2 * DFF),
                    (self.w_dn.rearrange("p a b -> p (a b)"), self.d_wdn, NFC * D)]
            n = 0
            for src, dst, tot in srcs:
                for o in range(0, tot, PIECE):
                    bi, ki, chi = wc_in.next()
                    bo, ko, cho = wc_out.next()
                    P.dma(chi, bi, src[:, o:o + PIECE], writes=[ki])
                    for _ in range(5):
                        yield
                    if n % 2 == 0:
                        P.add("act", lambda e, bi=bi, bo=bo: e.activation(out=bo, in_=bi, func=AF.Copy), reads=[ki], writes=[ko])
                    else:
                        P.add("dve", lambda e, bi=bi, bo=bo: e.tensor_copy(out=bo, in_=bi), reads=[ki], writes=[ko])
                    n += 1
                    for _ in range(5):
                        yield
                    P.dma(cho, dst[:, o:o + PIECE], bo, reads=[ko])
                    yield

        wb_in = self.mk_rot("wbin", [A.f32([PIECE]) for _ in range(2)])
        wb_msk = self.mk_rot("wbmsk", [A.f32([PIECE]) for _ in range(2)])
        wb_out = self.mk_rot("wbout", [A.bf([PIECE]) for _ in range(2)])

        def WB():
            wo_src = self.w_o.rearrange("p a b -> p (a b)")
            n = 0
            for o in range(0, 8 * D, PIECE):
                w_ = min(PIECE, 8 * D - o)
                bi, ki, chi = wb_in.next()
                bo, ko, cho = wb_out.next()
                P.dma(chi, bi[:, 0:w_], wo_src[:, o:o + w_], writes=[ki])
                for _ in range(5):
                    yield
                P.add("dve", lambda e, bi=bi, bo=bo, w_=w_: e.tensor_copy(out=bo[:, 0:w_], in_=bi[:, 0:w_]), reads=[ki], writes=[ko])
                for _ in range(5):
                    yield
                P.dma(cho, self.d_wo[:, o:o + w_], bo[:, 0:w_], reads=[ko])
                yield
            HB = NBLK * 128
            for h in range(8):
                for o in range(0, HB, PIECE):
                    w_ = min(PIECE, HB - o)
                    bi, ki, chi = wb_in.next()
                    bm, km, chm = wb_msk.next()
                    bo, ko, cho = wb_out.next()
                    P.dma(chi, bi[:, 0:w_], self.rpbg[:, h, o:o + w_], writes=[ki])
                    P.dma(chm, bm[:, 0:w_], self.amask[:, o:o + w_], writes=[km])
                    for _ in range(5):
                        yield
                    P.add("dve", lambda e, bi=bi, bm=bm, bo=bo, w_=w_: e.tensor_tensor(out=bo[:, 0:w_], in0=bi[:, 0:w_], in1=bm[:, 0:w_], op=ALU.add),
                          reads=[ki, km], writes=[ko])
                    for _ in range(5):
                        yield
                    P.dma(cho, self.d_bias[:, h * HB + o:h * HB + o + w_], bo[:, 0:w_], reads=[ko])
                    yield

        for j in range(4):
            load_x(j)
        for _ in LN_T(0):
            pass
        gw = WC() if "C" in self.stages else None
        gw2 = WB() if "B" in self.stages else None
        for m in range(self.nmacro):
            if m + 1 < self.nmacro:
                for j in range(4):
                    load_x(4 * (m + 1) + j)
            ga = PROJ(m)
            gb = LN_T(m + 1) if m + 1 < self.nmacro else None
            while ga is not None or gb is not None:
                if ga is not None and next(ga, "end") == "end":
                    ga = None
                if gb is not None and next(gb, "end") == "end":
                    gb = None
                if gw is not None and next(gw, "end") == "end":
                    gw = None
                if gw2 is not None and next(gw2, "end") == "end":
                    gw2 = None
        if gw is not None:
            for _ in gw:
                pass
        if gw2 is not None:
            for _ in gw2:
                pass

    def stage_B(self):
        P, A, pp, ps = self.P, self.A, self.pp, self.ps
        A.off = self.persist_end
        self._chan = 4
        wo = A.bf([8, D])
        bias = A.bf([8 * NBLK, 128])
        g1bc = A.f32([D]); b1bc = A.f32([D]); gn2bc = A.f32([512])
        mark = A.off
        P.dma(("ch", self.chan()), g1bc, self.ln1_g.partition_broadcast(128), writes=["g1bc"])
        P.dma(("ch", self.chan()), b1bc, self.ln1_b.partition_broadcast(128), writes=["b1bc"])
        P.dma(("ch", self.chan()), gn2bc, self.gn2.partition_broadcast(128), writes=["gn2bc"])
        c0 = self._chan; self._chan += 2
        for hlf in range(2):
            P.dma(("ch", c0 + hlf), wo[:, 4 * hlf:4 * hlf + 4, :].rearrange("p a b -> p (a b)"), self.d_wo[:, 4 * hlf * D:(4 * hlf + 4) * D], writes=[("wo", hlf)])
        c0 = self._chan; self._chan += 2
        for q in range(4):
            P.dma(("ch", c0 + q % 2), bias[:, 2 * q * NBLK:(2 * q + 2) * NBLK, :].rearrange("p a b -> p (a b)"),
                  self.d_bias[:, 2 * q * NBLK * 128:(2 * q + 2) * NBLK * 128], writes=[("bias", q)])
        P.barrier()
        A.off = mark
        qm_rot = self.mk_rot("qm", [A.bf([4, 512]) for _ in range(2)])
        kw_rot = self.mk_rot("kw", [A.bf([4, 640]) for _ in range(2)])
        vw_rot = self.mk_rot("vw", [A.bf([5, 520]) for _ in range(2)])
        sb_rot = self.mk_rot("sb", [A.f32([640]) for _ in range(3)], dma=False)
        pt_rot = self.mk_rot("pt", [A.bf([640]) for _ in range(4)], dma=False)
        ya_rot = self.mk_rot("ya", [A.f32([512]) for _ in range(2)], dma=False)
        yan_rot = self.mk_rot("yan", [A.bf([512]) for _ in range(2)], dma=False)
        yT_rot = self.mk_rot("yT", [A.bf([8, 512]) for _ in range(2)], dma=False)
        chw_rot = self.mk_rot("chw", [A.f32([4, 516]) for _ in range(1)])
        bgw_rot = self.mk_rot("bgw", [A.f32([4, 512]) for _ in range(1)])
        ab_rot = self.mk_rot("ab", [A.f32([512]) for _ in range(2)], dma=False)
        sq4 = A.f32([4, 512])
        sq_rot = self.mk_rot("sq", [A.f32([512])], dma=False)
        rbc = A.f32([512])
        rd_rot = self.mk_rot("rd", [A.f32([16]) for _ in range(3)], dma=False)
        xh_rot = self.mk_rot("xhB", [A.f32([D]) for _ in range(3)])
        x1_rot = self.mk_rot("x1", [A.f32([D]) for _ in range(2)])
        x1T_rot = self.mk_rot("x1T", [A.bf([8, 512]) for _ in range(2)])
        kTv = self.d_kT.rearrange("c p t -> p c t")
        qTv = self.d_qT.rearrange("c p t -> p c t")
        bgTv = self.d_bgT.rearrange("c p t -> p c t")
        chTv = self.d_chT.rearrange("c p t -> p c t")
        x1Tv = self.d_x1T.rearrange("c p t -> p c t")
        ntt = len(self.tt_info)
        mac = {}
        tts = {}

        mloads = {}

        def macro_loads(m):
            t0 = 512 * m
            sbase, R, _ = self.tt_info[4 * m]
            send = sbase + R * 64
            at_start = (t0 == sbase)
            at_end = (t0 + 512 == send)
            qm, qmk, qch = qm_rot.next()
            P.dma(qch, qm, qTv[:, :, t0:t0 + 512], writes=[qmk])
            chw, chwk, chch = chw_rot.next()
            bgw, bgwk, bgch = bgw_rot.next()
            lo_c = 1 if at_start else 0
            hi_c = 513 if at_end else 514
            if at_start:
                P.add("pool", lambda e, chw=chw: e.memset(chw[:, :, 0:1], 0.0), writes=[chwk])
            if at_end:
                P.add("pool", lambda e, chw=chw: e.memset(chw[:, :, 513:514], 0.0), writes=[chwk])
            P.dma(chch, chw[:, :, lo_c:hi_c], chTv[:, :, t0 - 1 + lo_c:t0 - 1 + hi_c], writes=[chwk])
            P.dma(bgch, bgw, bgTv[:, :, t0:t0 + 512], writes=[bgwk])
            mloads[m] = (qm, qmk, chw, chwk, bgw, bgwk)

        def macro_begin(m):
            t0 = 512 * m
            qm, qmk, chw, chwk, bgw, bgwk = mloads[m]
            yT, yTk, _ = yT_rot.next()
            x1T, x1Tk, x1Tch = x1T_rot.next()
            mac[m] = dict(qm=qm, qmk=qmk, yT=yT, yTk=yTk, x1T=x1T, x1Tk=x1Tk, x1Tch=x1Tch, t0=t0)

        def CM(m):
            qm, qmk, chw, chwk, bgw, bgwk = mloads[m]
            yT, yTk = mac[m]["yT"], mac[m]["yTk"]
            psum_sum = ps[:, 3584:4096]
            sqs = []
            for c in range(4):
                ab, abk, _ = ab_rot.next()
                P.add("dve", lambda e, ab=ab, c=c, chw=chw: e.tensor_scalar(out=ab, in0=chw[:, c, 1:513], scalar1=self.cw[:, c, 1:2], scalar2=self.cb[:, c:c + 1],
                                                                    op0=ALU.mult, op1=ALU.add), reads=[chwk], writes=[abk])
                P.add("dve", lambda e, ab=ab, c=c, chw=chw: e.scalar_tensor_tensor(out=ab, in0=chw[:, c, 0:512], scalar=self.cw[:, c, 0:1], in1=ab,
                                                                           op0=ALU.mult, op1=ALU.add), reads=[chwk, abk], writes=[abk])
                P.add("dve", lambda e, ab=ab, c=c, chw=chw: e.scalar_tensor_tensor(out=ab, in0=chw[:, c, 2:514], scalar=self.cw[:, c, 2:3], in1=ab,
                                                                           op0=ALU.mult, op1=ALU.add), reads=[chwk, abk], writes=[abk])
                yc_buf = bgw
                P.add("pool", lambda e, ab=ab, c=c, bgw=bgw: e.tensor_tensor(out=bgw[:, c, :], in0=ab, in1=bgw[:, c, :], op=ALU.mult),
                      reads=[abk, bgwk], writes=[("yc", c), bgwk])
                sq, sqk = sq4[:, c, :], ("sq4", c)
                P.add("act", lambda e, sq=sq, c=c, yc_buf=yc_buf: e.activation(out=sq, in_=yc_buf[:, c, :], func=AF.Square), reads=[("yc", c)], writes=[sqk])
                yield
                sqs.append((sq, sqk, c))
            for sq, sqk, c in sqs:
                P.add("pe", lambda e, sq=sq, c=c: e.matmul(psum_sum, lhsT=self.onesf, rhs=sq, start=(c == 0), stop=(c == 3)),
                      reads=[sqk, "onesf"], writes=[("ps", 7)])
            P.add("act", lambda e: e.activation(out=rbc, in_=psum_sum, func=AF.Ln, scale=1.0 / 512, bias=self.epsc[:, 0:1]),
                  reads=[("ps", 7)], writes=["rbc"])
            P.add("act", lambda e: e.activation(out=rbc, in_=rbc, func=AF.Exp, scale=-0.5), reads=["rbc"], writes=["rbc"])
            yield
            for c in range(4):
                P.add("dve", lambda e, c=c, yT=yT, yc_buf=yc_buf: e.scalar_tensor_tensor(out=yT[:, c, :], in0=yc_buf[:, c, :], scalar=self.gn1[:, c:c + 1], in1=rbc,
                                                                   op0=ALU.mult, op1=ALU.mult), reads=[("yc", c), "rbc", bgwk], writes=[yTk])
                yield

        loads = {}

        def tt_loads(g):
            sb_, R, r0 = self.tt_info[g]
            var, lo, nch = tt_geom(R, r0)
            nk = 128 * nch
            ktok = sb_ + 64 * lo
            kw, kwk, kch = kw_rot.next()
            vw, vwk, vch = vw_rot.next()
            P.dma(kch, kw[:, :, 0:nk], kTv[:, :, ktok:ktok + nk], writes=[kwk])
            P.dma(vch, vw[:, 0:nch, :], self.d_va[ktok:ktok + nk, :].rearrange("(c p) d -> p c d", p=128), writes=[vwk])
            xh, xhk, xhch = xh_rot.next()
            P.dma(xhch, xh, self.d_xh[g * 128:(g + 1) * 128, :], writes=[xhk])
            loads[g] = (kw, kwk, vw, vwk, xh, xhk)

        def P1(g):
            m, j = g // 4, g % 4
            M = mac[m]
            qm, qmk = M["qm"], M["qmk"]
            sb_, R, r0 = self.tt_info[g]
            var, lo, nch = tt_geom(R, r0)
            nk = 128 * nch
            ktok = sb_ + 64 * lo
            kw, kwk, vw, vwk, xh, xhk = loads[g]
            if g + 1 < ntt:
                tt_loads(g + 1)
            pts = [None] * 8

            def geom(h):
                if h % 2 == 0:
                    start = 0
                    pos = lambda c: c
                    keys = [("ps", 0)] + ([("ps", 1)] if nch == 5 else [])
                    slot0 = VAR_BASE[var]
                else:
                    if nch == 5:
                        start = 896
                        pos = lambda c: (c + 1) if c < 4 else 0
                        slot0 = 0
                    else:
                        start = 1024
                        pos = lambda c: c
                        slot0 = VAR_BASE[var]
                    keys = [("ps", 2)] + ([("ps", 1)] if nch == 5 else [])
                return start, pos, keys, slot0

            def emit_S(h):
                p_, hf = h // 2, h % 2
                start, pos, keys, slot0 = geom(h)
                for c in range(nch):
                    col = start + 128 * pos(c)
                    bk_ = col // 512
                    P.add("pe", lambda e, col=col, c=c, p_=p_, hf=hf: e.matmul(ps[:, col:col + 128], lhsT=kw[64 * hf:64 * hf + 64, p_, c * 128:(c + 1) * 128],
                                                                        rhs=qm[64 * hf:64 * hf + 64, p_, 128 * j:128 * (j + 1)], start=True, stop=not BIAS_ON_PE),
                          reads=[kwk, qmk], writes=[("ps", bk_)])
                    if BIAS_ON_PE:
                        blk = h * NBLK + VAR_BASE[var] + c
                        P.add("pe", lambda e, col=col, blk=blk: e.matmul(ps[:, col:col + 128], lhsT=self.idb, rhs=bias[:, blk, :], start=False, stop=True),
                              reads=[], writes=[("ps", bk_)])
                pt, ptk, _ = pt_rot.next()
                if BIAS_ON_PE:
                    P.add("act", lambda e, pt=pt, start=start: e.activation(out=pt[:, 0:nk], in_=ps[:, start:start + nk], func=AF.Exp), reads=keys, writes=[ptk])
                else:
                    sbf, sbk, _ = sb_rot.next()
                    blk0 = h * NBLK + slot0
                    P.add("dve", lambda e, sbf=sbf, start=start, blk0=blk0: e.tensor_tensor(out=sbf[:, 0:nk], in0=ps[:, start:start + nk],
                                                                                   in1=bias[:, blk0:blk0 + nch, :].rearrange("p a b -> p (a b)"), op=ALU.add),
                          reads=keys, writes=[sbk])
                    P.add("act", lambda e, pt=pt, sbf=sbf: e.activation(out=pt[:, 0:nk], in_=sbf[:, 0:nk], func=AF.Exp), reads=[sbk], writes=[ptk])
                pts[h] = (pt, ptk, pos)

            def emit_PV(h):
                pt, ptk, pos = pts[h]
                if h < 7:
                    ocol, ob = 1536 + 65 * h, 3
                else:
                    ocol, ob = 2048, 4
                for c in range(nch):
                    pc = 128 * pos(c)
                    P.add("pe", lambda e, c=c, pc=pc, pt=pt, h=h, ocol=ocol: e.matmul(ps[:, ocol:ocol + 65], lhsT=pt[:, pc:pc + 128], rhs=vw[:, c, h * 65:(h + 1) * 65],
                                                                              start=(c == 0), stop=(c == nch - 1)),
                          reads=[ptk, vwk], writes=[("ps", ob)])
            for h in range(S_LOOKAHEAD):
                emit_S(h)
            yield
            for h in range(8):
                if h + S_LOOKAHEAD < 8:
                    emit_S(h + S_LOOKAHEAD)
                emit_PV(h)
                yield
            rd, rdk, _ = rd_rot.next()
            ya, yak, _ = ya_rot.next()
            Ob = ps[:, 1536:1536 + 455].rearrange("p (h d) -> p h d", h=7)
            P.add("dve", lambda e: e.reciprocal(out=rd[:, 0:7], in_=Ob[:, :, 64]), reads=[("ps", 3)], writes=[rdk])
            P.add("dve", lambda e: e.reciprocal(out=rd[:, 7:8], in_=ps[:, 2112:2113]), reads=[("ps", 4)], writes=[rdk])
            P.add("dve", lambda e: e.tensor_tensor(out=ya[:, 0:448].rearrange("p (h d) -> p h d", h=7), in0=Ob[:, :, 0:64],
                                                   in1=rd[:, 0:7].unsqueeze(2).broadcast_to([128, 7, 64]), op=ALU.mult),
                  reads=[("ps", 3), rdk], writes=[yak])
            P.add("dve", lambda e: e.tensor_scalar(out=ya[:, 448:512], in0=ps[:, 2048:2112], scalar1=rd[:, 7:8], scalar2=None, op0=ALU.mult),
                  reads=[("ps", 4), rdk], writes=[yak])
            sq, sqk, _ = sq_rot.next()
            P.add("act", lambda e: e.activation(out=sq, in_=ya, func=AF.Square, accum_out=rd[:, 8:9]), reads=[yak], writes=[sqk, rdk])
            P.add("act", lambda e: e.activation(out=rd[:, 9:10], in_=rd[:, 8:9], func=AF.Ln, scale=1.0 / 512, bias=self.epsc[:, 0:1]),
                  reads=[rdk], writes=[rdk])
            P.add("act", lambda e: e.activation(out=rd[:, 10:11], in_=rd[:, 9:10], func=AF.Exp, scale=-0.5), reads=[rdk], writes=[rdk])
            yan, yank, _ = yan_rot.next()
            P.add("dve", lambda e: e.scalar_tensor_tensor(out=yan, in0=ya, scalar=rd[:, 10:11], in1=gn2bc, op0=ALU.mult, op1=ALU.mult),
                  reads=[yak, rdk, "gn2bc"], writes=[yank])
            tts[g] = dict(m=m, j=j, xh=xh, xhk=xhk, yan=yan, yank=yank)
            yield

        def P2(g):
            T = tts[g]
            M = mac[T["m"]]
            j, xh, xhk, yan, yank = T["j"], T["xh"], T["xhk"], T["yan"], T["yank"]
            yT, yTk = M["yT"], M["yTk"]
            tpb = ps[:, 2048 + 128:2048 + 384].bitcast(BF16)
            for c in range(4):
                P.add("pe", lambda e, c=c: e.transpose(out=tpb[:, c * 128:(c + 1) * 128], in_=yan[:, c * 128:(c + 1) * 128], identity=self.idb),
                      reads=[yank], writes=[("ps", 4)])
            P.add("act", lambda e: e.activation(out=yT[:, 4:8, 128 * j:128 * (j + 1)], in_=tpb.rearrange("p (a b) -> p a b", a=4), func=AF.Copy),
                  reads=[("ps", 4)], writes=[yTk])
            yield
            for hlf in range(2):
                W = ps[:, 2560 + 512 * hlf:3072 + 512 * hlf]
                for kc in range(8):
                    P.add("pe", lambda e, hlf=hlf, kc=kc, W=W: e.matmul(W, lhsT=yT[:, kc, 128 * j:128 * (j + 1)],
                                                                  rhs=wo[:, kc, hlf * 512:(hlf + 1) * 512], start=(kc == 0), stop=(kc == 7)),
                          reads=[yTk], writes=[("ps", 5 + hlf)])
                yield
            for hlf in range(2):
                W = ps[:, 2560 + 512 * hlf:3072 + 512 * hlf]
                P.add("dve", lambda e, hlf=hlf, W=W: e.scalar_tensor_tensor(out=xh[:, hlf * 512:(hlf + 1) * 512], in0=xh[:, hlf * 512:(hlf + 1) * 512], scalar=ALPHA,
                                                                      in1=W, op0=ALU.mult, op1=ALU.add),
                      reads=[xhk, ("ps", 5 + hlf)], writes=[xhk])
            yield
            x1, x1k, x1ch = x1_rot.next()
            for _ in self.layernorm_gen(xh, xhk, x1, x1k, g1bc, b1bc, "g1bc", "b1bc", mode="lnexp", aff=LN1_AFF):
                yield
            P.dma(x1ch, self.d_x1[g * 128:(g + 1) * 128, :], x1, reads=[x1k])
            T["x1"], T["x1k"] = x1, x1k
            yield

        def P3(g):
            T = tts[g]
            M = mac[T["m"]]
            j, x1, x1k = T["j"], T["x1"], T["x1k"]
            x1T, x1Tk = M["x1T"], M["x1Tk"]
            tp = ps[:, 3584:4096]
            for r in range(2):
                for c in range(4):
                    cc = 4 * r + c
                    P.add("pe", lambda e, c=c, cc=cc: e.transpose(out=tp[:, c * 128:(c + 1) * 128], in_=x1[:, cc * 128:(cc + 1) * 128], identity=self.idf),
                          reads=[x1k], writes=[("ps", 7)])
                dst = x1T[:, 4 * r:4 * r + 4, 128 * j:128 * (j + 1)]
                src = tp.rearrange("p (a b) -> p a b", a=4)
                if r == 0:
                    P.add("act", lambda e, dst=dst, src=src: e.activation(out=dst, in_=src, func=AF.Copy), reads=[("ps", 7)], writes=[x1Tk])
                else:
                    P.add("dve", lambda e, dst=dst, src=src: e.tensor_copy(out=dst, in_=src), reads=[("ps", 7)], writes=[x1Tk])
                yield
            if j == 3:
                P.dma(M["x1Tch"], x1Tv[:, :, M["t0"]:M["t0"] + 512], x1T, reads=[x1Tk])
                if self.debug:
                    P.dma(("ch", 90 + T["m"] % 2), self.d_yT.rearrange("c p t -> p c t")[:, :, M["t0"]:M["t0"] + 512], M["yT"], reads=[M["yTk"]])

        macro_loads(0)
        macro_begin(0)
        for _ in CM(0):
            pass
        if self.nmacro > 1:
            macro_loads(1)
        tt_loads(0)
        g0 = None
        for g in range(ntt + 2):
            g1 = P1(g) if g < ntt else None
            g2 = P2(g - 1) if 0 <= g - 1 < ntt else None
            g3 = P3(g - 2) if 0 <= g - 2 < ntt else None
            if g % 4 == 1 and g // 4 + 1 < self.nmacro:
                macro_begin(g // 4 + 1)
                g0 = CM(g // 4 + 1)
            step = 0
            while g1 is not None or g2 is not None or g3 is not None:
                if g1 is not None and next(g1, "end") == "end":
                    g1 = None
                if g3 is not None and step >= 1 and next(g3, "end") == "end":
                    g3 = None
                if g2 is not None and (step >= P2_START or g1 is None) and next(g2, "end") == "end":
                    g2 = None
                if g0 is not None and step % 2 == 1 and next(g0, "end") == "end":
                    g0 = None
                step += 1
            if g % 4 == 3:
                if g0 is not None:
                    for _ in g0:
                        pass
                    g0 = None
                if g // 4 + 2 < self.nmacro:
                    macro_loads(g // 4 + 2)

    def stage_C(self):
        P, A, pp = self.P, self.A, self.pp
        A.off = self.persist_end
        self._chan = 4
        wup = A.bf([8, 2 * DFF])
        wdn = A.bf([NFC, D])
        g2bc = A.f32([D]); b2bc = A.f32([D])
        mark = A.off
        P.dma(("ch", self.chan()), g2bc, self.ln2_g.partition_broadcast(128), writes=["g2bc"])
        P.dma(("ch", self.chan()), b2bc, self.ln2_b.partition_broadcast(128), writes=["b2bc"])
        c0 = self._chan; self._chan += 2
        for kc in range(8):
            P.dma(("ch", c0 + kc % 2), wup[:, kc, :], self.d_wup[:, kc * 2 * DFF:(kc + 1) * 2 * DFF], writes=[("wup", kc)])
        c0 = self._chan; self._chan += 2
        for q in range(2):
            P.dma(("ch", c0 + q), wdn[:, 11 * q:11 * q + 11, :].rearrange("p a b -> p (a b)"), self.d_wdn[:, 11 * q * D:(11 * q + 11) * D], writes=[("wdn", q)])
        P.barrier()
        A.off = mark
        xw_rot = self.mk_rot("xw", [A.bf([8, 516]) for _ in range(1)])
        hT = A.bf([NFC, 512])
        halo_sb = A.f32([88])
        U_rot = self.mk_rot("U", [A.f32([516]) for _ in range(3)], dma=False)
        ab_rot = self.mk_rot("abC", [A.f32([512]) for _ in range(6)], dma=False)
        x1_rot = self.mk_rot("x1C", [A.f32([D]) for _ in range(3)])
        x1Tv = self.d_x1T.rearrange("c p t -> p c t")
        gelu = getattr(AF, GELU_FUNC)
        bank_rr = 0
        xws = {}
        x1s = {}

        def load_xw(m):
            t0 = 512 * m
            sbase, R, _ = self.tt_info[4 * m]
            send = sbase + R * 64
            at_start = (t0 == sbase)
            at_end = (t0 + 512 == send)
            xw, xwk, xwch = xw_rot.next()
            lo_c = 1 if at_start else 0
            hi_c = 513 if at_end else 514
            if at_start:
                P.add("pool", lambda e, xw=xw: e.memset(xw[:, :, 0:1], 0.0), writes=[xwk])
            if at_end:
                P.add("pool", lambda e, xw=xw: e.memset(xw[:, :, 513:514], 0.0), writes=[xwk])
            P.dma(xwch, xw[:, :, lo_c:hi_c], x1Tv[:, :, t0 - 1 + lo_c:t0 - 1 + hi_c], writes=[xwk])
            xws[m] = (xw, xwk)

        x1chs = {}

        def load_x1(g):
            x1, x1k, x1ch = x1_rot.next()
            P.dma(x1ch, x1, self.d_x1[g * 128:(g + 1) * 128, :], writes=[x1k])
            x1s[g] = (x1, x1k)
            x1chs[g] = x1ch

        for m in range(self.nmacro):
            t0 = 512 * m
            if m == 0:
                load_xw(0)
            xw, xwk = xws[m]
            hps = pp[2][:, 0:88]
            for cc in range(44):
                for kc in range(8):
                    P.add("pe", lambda e, cc=cc, kc=kc, xw=xw: e.matmul(hps[:, 2 * cc:2 * cc + 2], lhsT=wup[:, kc, cc * 128:(cc + 1) * 128], rhs=xw[:, kc, 0:514:513],
                                                                start=(kc == 0), stop=(kc == 7)),
                          reads=[xwk], writes=[("ps", 4)])
            P.add("act", lambda e: e.activation(out=halo_sb, in_=hps, func=AF.Copy), reads=[("ps", 4)], writes=["halo"])
            pend_gelu = []
            pend_q = []
            for f in range(NFC):
                info = []
                for which in range(2):
                    cc = f + which * NFC
                    bank = bank_rr % 4
                    bank_rr += 1
                    pso = pp[bank // 2][:, (bank % 2) * 512:(bank % 2 + 1) * 512]
                    for kc in range(8):
                        P.add("pe", lambda e, pso=pso, cc=cc, kc=kc, xw=xw: e.matmul(pso, lhsT=wup[:, kc, cc * 128:(cc + 1) * 128], rhs=xw[:, kc, 1:513],
                                                                             start=(kc == 0), stop=(kc == 7)),
                              reads=[xwk], writes=[("ps", bank)])
                    U, Uk, _ = U_rot.next()
                    ab, abk, _ = ab_rot.next()
                    P.add("act", lambda e, U=U, pso=pso: e.activation(out=U[:, 1:513], in_=pso, func=AF.Copy), reads=[("ps", bank)], writes=[Uk])
                    P.add("act", lambda e, ab=ab, pso=pso, cc=cc: e.activation(out=ab, in_=pso, func=AF.Identity, scale=self.fw[:, cc, 1:2], bias=self.fb[:, cc:cc + 1]),
                          reads=[("ps", bank)], writes=[abk])
                    info.append((U, Uk, ab, abk, cc))
                if pend_gelu:
                    ag, agk, av, avk, fp = pend_gelu.pop(0)
                    P.add("act", lambda e, ag=ag: e.activation(out=ag, in_=ag, func=gelu), reads=[agk], writes=[agk])
                    pend_q.append((ag, agk, av, avk, fp))
                for (U, Uk, ab, abk, cc) in info:
                    P.add("dve", lambda e, U=U, cc=cc: e.tensor_copy(out=U[:, 0:514:513], in_=halo_sb[:, 2 * cc:2 * cc + 2]), reads=["halo"], writes=[Uk])
                for (U, Uk, ab, abk, cc) in info:
                    P.add("dve", lambda e, ab=ab, U=U, cc=cc: e.scalar_tensor_tensor(out=ab, in0=U[:, 0:512], scalar=self.fw[:, cc, 0:1], in1=ab,
                                                                             op0=ALU.mult, op1=ALU.add), reads=[Uk, abk], writes=[abk])
                if len(pend_q) > 1 or (pend_q and not pend_gelu and False):
                    ag, agk, av, avk, fp = pend_q.pop(0)
                    P.add("dve", lambda e, ag=ag, av=av, fp=fp: e.tensor_tensor(out=hT[:, fp, :], in0=ag, in1=av, op=ALU.mult), reads=[agk, avk], writes=[("hT", fp)])
                for (U, Uk, ab, abk, cc) in info:
                    P.add("dve", lambda e, ab=ab, U=U, cc=cc: e.scalar_tensor_tensor(out=ab, in0=U[:, 2:514], scalar=self.fw[:, cc, 2:3], in1=ab,
                                                                             op0=ALU.mult, op1=ALU.add), reads=[Uk, abk], writes=[abk])
                (_, _, ag, agk, _), (_, _, av, avk, _) = info
                pend_gelu.append((ag, agk, av, avk, f))
            while pend_gelu:
                ag, agk, av, avk, fp = pend_gelu.pop(0)
                P.add("act", lambda e, ag=ag: e.activation(out=ag, in_=ag, func=gelu), reads=[agk], writes=[agk])
                pend_q.append((ag, agk, av, avk, fp))
            while pend_q:
                ag, agk, av, avk, fp = pend_q.pop(0)
                P.add("dve", lambda e, ag=ag, av=av, fp=fp: e.tensor_tensor(out=hT[:, fp, :], in0=ag, in1=av, op=ALU.mult), reads=[agk, avk], writes=[("hT", fp)])
            if m + 1 < self.nmacro:
                load_xw(m + 1)
            load_x1(4 * m)
            load_x1(4 * m + 1)
            for j in range(4):
                g = 4 * m + j
                x1, x1k = x1s[g]
                banks = (5, 6) if j % 2 == 0 else (7, 4)
                outs = [pp[b // 2][:, (b % 2) * 512:(b % 2 + 1) * 512] for b in banks]
                for hlf in range(2):
                    for f in range(NFC):
                        P.add("pe", lambda e, hlf=hlf, f=f, j=j, o=outs[hlf]: e.matmul(o, lhsT=hT[:, f, 128 * j:128 * (j + 1)], rhs=wdn[:, f, hlf * 512:(hlf + 1) * 512],
                                                                           start=(f == 0), stop=(f == NFC - 1)),
                              reads=[("hT", f)], writes=[("ps", banks[hlf])])
                for hlf in range(2):
                    P.add("dve", lambda e, hlf=hlf, x1=x1, o=outs[hlf]: e.scalar_tensor_tensor(out=x1[:, hlf * 512:(hlf + 1) * 512], in0=x1[:, hlf * 512:(hlf + 1) * 512], scalar=ALPHA,
                                                                                  in1=o, op0=ALU.mult, op1=ALU.add),
                          reads=[x1k, ("ps", banks[hlf])], writes=[x1k])
                if j + 2 < 4:
                    load_x1(g + 2)
                self.layernorm(x1, x1k, x1, x1k, g2bc, b2bc, "g2bc", "b2bc")
                P.dma(x1chs[g], self.y[g * 128:(g + 1) * 128, :], x1, reads=[x1k], final=True)


_NC_CACHE = {}


def prep_shared(inputs):
    f = lambda a: np.ascontiguousarray(np.asarray(a, dtype=np.float32))
    w_in = f(inputs["w_in"])[0].reshape(8, 128, DP).transpose(1, 0, 2)
    w_o = f(inputs["w_o"])[0].reshape(8, 128, D).transpose(1, 0, 2)
    w_up = f(inputs["w_up"])[0].reshape(8, 128, 2 * DFF).transpose(1, 0, 2)
    w_dn = f(inputs["w_down"])[0].reshape(NFC, 128, D).transpose(1, 0, 2)
    smallp = np.zeros((128, 256), np.float32)
    cw = f(inputs["conv_w"])[0]
    smallp[:, 0:12] = cw.reshape(3, 4, 128).transpose(2, 1, 0).reshape(128, 12)
    smallp[:, 12:16] = f(inputs["conv_b"])[0].reshape(4, 128).T
    gn = f(inputs["gn_g"])[0]
    smallp[:, 16:20] = gn[:512].reshape(4, 128).T
    fw = f(inputs["ffn_conv_w"])[0]
    smallp[:, 20:152] = fw.reshape(3, 44, 128).transpose(2, 1, 0).reshape(128, 132)
    smallp[:, 152:196] = f(inputs["ffn_conv_b"])[0].reshape(44, 128).T
    dr_idx, dc_idx, valid = bias_index_tables()
    rpb = f(inputs["rpb"])[0]
    g_ = rpb[:, dr_idx, dc_idx]
    rpbg = np.ascontiguousarray(g_.transpose(2, 0, 1, 3).reshape(128, 8, NBLK * 128))
    amask = np.ascontiguousarray(np.where(valid, np.float32(0.0), np.float32(NEG)).transpose(1, 0, 2).reshape(128, NBLK * 128)).astype(np.float32)
    sh = {
        "w_in": np.ascontiguousarray(w_in), "w_o": np.ascontiguousarray(w_o),
        "w_up": np.ascontiguousarray(w_up), "w_dn": np.ascontiguousarray(w_dn),
        "ln_in_g": f(inputs["ln_in_g"]), "ln_in_b": f(inputs["ln_in_b"]),
        "ln1_g": f(inputs["ln1_g"])[0], "ln1_b": f(inputs["ln1_b"])[0],
        "ln2_g": f(inputs["ln2_g"])[0], "ln2_b": f(inputs["ln2_b"])[0],
        "gn2": np.ascontiguousarray(gn[512:]),
        "smallp": smallp, "ident": np.eye(128, dtype=np.float32),
        "rpbg": rpbg, "amask": amask,
    }
    return sh


def kernel(**inputs):
    xp = np.asarray(inputs["x_prompt"], dtype=np.float32)
    xs = np.asarray(inputs["x_sample"], dtype=np.float32)
    n = 8
    bp, sp_, _ = xp.shape
    bs, ss_, _ = xs.shape
    pp_ = bp // n
    ps_ = bs // n
    seq_lens = [sp_] * pp_ + [ss_] * ps_
    key = tuple(seq_lens)
    if key not in _NC_CACHE:
        _NC_CACHE[key] = Builder(seq_lens).build()
    nc = _NC_CACHE[key]
    sh = prep_shared(inputs)
    in_maps = []
    for c in range(n):
        xc = np.concatenate([xp[c * pp_:(c + 1) * pp_].reshape(-1, D), xs[c * ps_:(c + 1) * ps_].reshape(-1, D)], axis=0)
        d = dict(sh)
        d["x"] = np.ascontiguousarray(xc)
        in_maps.append(d)
    res = run_bass_kernel_spmd(nc, in_maps, core_ids=list(range(n)))
    yp = np.empty_like(xp)
    ys = np.empty_like(xs)
    for c in range(n):
        yc = np.asarray(res.results[c]["y"], dtype=np.float32)
        yp[c * pp_:(c + 1) * pp_] = yc[:pp_ * sp_].reshape(pp_, sp_, D)
        ys[c * ps_:(c + 1) * ps_] = yc[pp_ * sp_:].reshape(ps_, ss_, D)
    return (yp, ys)
```

```python
import numpy as np
from contextlib import ExitStack
import concourse.bass as bass
import concourse.mybir as mybir
from concourse.bass_utils import run_bass_kernel_spmd

F32 = mybir.dt.float32
BF16 = mybir.dt.bfloat16
AF = mybir.ActivationFunctionType
ALU = mybir.AluOpType

D = 1024
DP = 3072
DFF = 2816
NFC = 22
GRID_W = 64
ALPHA = float(2.0 ** 0.25)
EPS = 1e-5
NEG = -1e30
ENGS = ("pe", "act", "dve", "pool", "sp")
GELU_FUNC = "Gelu_apprx_tanh"
BIAS_ON_PE = False
S_LOOKAHEAD = 3
P2_START = 1
LN1_AFF = "pool"


class Op:
    __slots__ = ("eng", "fn", "waits", "sig", "is_dma", "chan", "idx", "sigval")

    def __init__(self, eng, fn, is_dma=False, chan=None):
        self.eng = eng
        self.fn = fn
        self.waits = {}
        self.sig = False
        self.is_dma = is_dma
        self.chan = chan
        self.sigval = None


class Prog:
    def __init__(self, nc):
        self.nc = nc
        self.ops = {e: [] for e in ENGS}
        self.last_w = {}
        self.readers = {}
        self.chan_last = {}
        self.chan_cnt = {}
        self.final_dmas = []

    def _dep(self, op, d):
        if d is op:
            return
        if d.is_dma:
            key = ("c", d.chan)
            val = d.sigval
            cur = op.waits.get(key)
            if cur is None or val > cur:
                op.waits[key] = val
            return
        if d.eng == op.eng and not op.is_dma and d.eng == "pe":
            return
        d.sig = True
        key = ("e", d.eng)
        cur = op.waits.get(key)
        if cur is None or d.idx > cur.idx:
            op.waits[key] = d

    def add(self, eng, fn, reads=(), writes=(), is_dma=False, chan=None):
        op = Op(eng, fn, is_dma, chan)
        op.idx = len(self.ops[eng])
        if is_dma:
            n = self.chan_cnt.get(chan, 0) + 1
            self.chan_cnt[chan] = n
            op.sigval = 16 * n
            prev = self.chan_last.get(chan)
            if prev is not None:
                self._dep(op, prev)
            self.chan_last[chan] = op
        for r in reads:
            w = self.last_w.get(r)
            if w is not None:
                self._dep(op, w)
        for r in writes:
            w = self.last_w.get(r)
            if w is not None and (w.is_dma or op.is_dma or w.eng != op.eng):
                self._dep(op, w)
            for rd in self.readers.get(r, ()):
                if rd.is_dma or op.is_dma or rd.eng != op.eng:
                    self._dep(op, rd)
        for r in reads:
            self.readers.setdefault(r, []).append(op)
        for r in writes:
            self.last_w[r] = op
            self.readers[r] = []
        self.ops[eng].append(op)
        return op

    def dma(self, chan, out, in_, reads=(), writes=(), final=False, eng="sp"):
        op = self.add(eng, lambda e: e.dma_start(out=out, in_=in_), reads, writes, is_dma=True, chan=chan)
        if final:
            self.final_dmas.append(op)
        return op

    def barrier(self):
        lasts = []
        for e in ENGS:
            for op in reversed(self.ops[e]):
                if not op.is_dma and op.fn is not None:
                    lasts.append(op)
                    break
        lasts += list(self.chan_last.values())
        for e in ENGS:
            op = Op(e, None)
            op.idx = len(self.ops[e])
            for d in lasts:
                if d.is_dma:
                    self._dep(op, d)
                elif d.eng != e:
                    self._dep(op, d)
                elif e not in ("pe",):
                    self._dep(op, d)
            self.ops[e].append(op)
        self.last_w = {}
        self.readers = {}

    def emit(self, stack):
        nc = self.nc
        for e in ENGS:
            n = 0
            for op in self.ops[e]:
                if not op.is_dma and op.sig:
                    n += 1
                    op.sigval = n
        esem = {e: stack.enter_context(nc.semaphore("s_" + e)) for e in ENGS if e != "sp"}
        csem = {}
        for i, ch in enumerate(self.chan_cnt):
            csem[ch] = stack.enter_context(nc.semaphore("c%d" % i))
        self.nsem = len(esem) + len(csem)
        block = stack.enter_context(nc.Block())
        finals = self.final_dmas

        def run(engname):
            def body(eng):
                waited = {}
                for op in self.ops[engname]:
                    for key, val in op.waits.items():
                        if key[0] == "c":
                            sem = csem[key[1]]
                            v = val
                        else:
                            sem = esem[key[1]]
                            v = val.sigval
                        if waited.get(key, 0) >= v:
                            continue
                        waited[key] = v
                        eng.wait_ge(sem, v)
                    if op.fn is None:
                        continue
                    ins = op.fn(eng)
                    if op.is_dma:
                        ins.then_inc(csem[op.chan], 16)
                    elif op.sig:
                        ins.then_inc(esem[engname], 1)
                if engname == "sp":
                    for ch, cnt in self.chan_cnt.items():
                        eng.wait_ge(csem[ch], 16 * cnt)
            return body

        block.tensor(run("pe"))
        block.scalar(run("act"))
        block.vector(run("dve"))
        block.gpsimd(run("pool"))
        block.sync(run("sp"))


class Rot:
    def __init__(self, name, bufs, chan0=None):
        self.name = name
        self.b = bufs
        self.i = 0
        self.chan0 = chan0

    def next(self):
        k = self.i % len(self.b)
        self.i += 1
        ch = None if self.chan0 is None else ("ch", self.chan0 + k)
        return self.b[k], (self.name, k), ch


class Arena:
    def __init__(self, nc, stack, nbytes):
        self.t = stack.enter_context(nc.sbuf_tensor("arena", [128, nbytes // 4], F32))
        self.cap = nbytes
        self.off = 0

    def _alloc(self, nbytes):
        nbytes = (nbytes + 63) // 64 * 64
        o = self.off
        self.off += nbytes
        assert self.off <= self.cap, ("SBUF arena overflow", self.off, self.cap)
        return o

    def f32(self, shape):
        n = int(np.prod(shape))
        o = self._alloc(n * 4)
        v = self.t[:, o // 4:o // 4 + n]
        return self._shape(v, shape)

    def bf(self, shape):
        n = int(np.prod(shape))
        assert n % 2 == 0
        o = self._alloc(n * 2)
        v = self.t[:, o // 4:o // 4 + n // 2].bitcast(BF16)
        return self._shape(v, shape)

    @staticmethod
    def _shape(v, shape):
        if len(shape) == 1:
            return v
        if len(shape) == 2:
            return v.rearrange("p (a b) -> p a b", a=shape[0])
        if len(shape) == 3:
            return v.rearrange("p (a b c) -> p a b c", a=shape[0], b=shape[1])
        raise ValueError(shape)


VARIANTS = ("int", "top0", "top2", "botA", "botB")
VAR_NCH = {"int": 5, "top0": 4, "top2": 4, "botA": 4, "botB": 4}
VAR_BASE = {"int": 1, "top0": 6, "top2": 10, "botA": 14, "botB": 18}
NBLK = 22


def rs_of(r, R):
    return min(max(r - 4, 0), R - 8)


def tt_geom(R, r0):
    lo = rs_of(r0, R)
    hi = rs_of(r0 + 1, R) + 8
    nrows = hi - lo
    nch = (nrows + 1) // 2
    if r0 == 0:
        v = "top0"
    elif r0 == 2:
        v = "top2"
    elif r0 == R - 4:
        v = "botA"
    elif r0 == R - 2:
        v = "botB"
    else:
        v = "int"
    if R == 8:
        v = {0: "top0", 2: "top2", 4: "botA", 6: "botB"}[r0]
    assert VAR_NCH[v] == nch, (R, r0, v, nch)
    return v, lo, nch


def bias_index_tables():
    R = 32
    r0_of = {"int": 8, "top0": 0, "top2": 2, "botA": R - 4, "botB": R - 2}
    dr_idx = np.zeros((NBLK, 128, 128), np.int64)
    dc_idx = np.zeros((NBLK, 128, 128), np.int64)
    valid = np.zeros((NBLK, 128, 128), bool)
    cols = np.arange(GRID_W)
    cstart = np.clip(cols - 8, 0, GRID_W - 16)
    for v in VARIANTS:
        r0 = r0_of[v]
        var, lo, nch = tt_geom(R, r0)
        assert var == v
        for c in range(nch):
            b = VAR_BASE[v] + c
            dup = (v == "int" and c == 4)
            for jr in range(2):
                for qr in range(2):
                    j = lo + 2 * c + jr
                    r = r0 + qr
                    rsr = rs_of(r, R)
                    rowok = (rsr <= j < rsr + 8)
                    kc = cols[:, None]
                    qc = cols[None, :]
                    colok = (kc >= cstart[None, :]) & (kc < cstart[None, :] + 16)
                    dri = np.clip(j - r + 7, 0, 14)
                    dci = np.clip(kc - qc, -15, 15) + 15
                    sl = (b, slice(jr * 64, jr * 64 + 64), slice(qr * 64, qr * 64 + 64))
                    dr_idx[sl] = dri
                    dc_idx[sl] = dci
                    valid[sl] = colok & rowok
                    if dup:
                        sl0 = (0,) + sl[1:]
                        dr_idx[sl0] = dri
                        dc_idx[sl0] = dci
                        valid[sl0] = colok & rowok
    return dr_idx, dc_idx, valid


class Builder:
    def __init__(self, seq_lens, debug=False, stages="ABC"):
        self.seq_lens = list(seq_lens)
        assert all(s % 512 == 0 and s >= 512 for s in self.seq_lens)
        self.ntok = sum(self.seq_lens)
        self.debug = debug
        self.stages = stages
        self.tt_info = []
        base = 0
        for s in self.seq_lens:
            R = s // 64
            for i in range(s // 128):
                self.tt_info.append((base, R, 2 * i))
            base += s
        self.nmacro = self.ntok // 512

    def chan(self):
        self._chan += 1
        return self._chan - 1

    def mk_rot(self, name, bufs, dma=True):
        c0 = None
        if dma:
            c0 = self._chan
            self._chan += len(bufs)
        return Rot(name, bufs, c0)

    def layernorm(self, z, zk, out, outk, gbc, bbc, gk, bk, mode="pow"):
        for _ in self.layernorm_gen(z, zk, out, outk, gbc, bbc, gk, bk, mode):
            pass

    def layernorm_gen(self, z, zk, out, outk, gbc, bbc, gk, bk, mode="pow", aff="pool"):
        P = self.P
        st, stk, _ = self.st_rot.next()
        P.add("dve", lambda e: e.bn_stats(out=st[:, 0:6], in_=z[:, 0:512]), reads=[zk], writes=[stk])
        P.add("dve", lambda e: e.bn_stats(out=st[:, 6:12], in_=z[:, 512:1024]), reads=[zk], writes=[stk])
        P.add("dve", lambda e: e.bn_aggr(out=st[:, 12:14], in_=st[:, 0:12]), reads=[stk], writes=[stk])
        yield
        if mode == "pow":
            P.add("dve", lambda e: e.tensor_scalar(out=st[:, 14:15], in0=st[:, 13:14], scalar1=EPS, scalar2=None, op0=ALU.add),
                  reads=[stk], writes=[stk])
            P.add("pool", lambda e: e.tensor_tensor(out=st[:, 15:16], in0=st[:, 14:15], in1=self.nh[:, 0:1], op=ALU.pow),
                  reads=[stk], writes=[stk])
        elif mode == "sqrt":
            P.add("act", lambda e: e.activation(out=st[:, 14:15], in_=st[:, 13:14], func=AF.Sqrt, bias=self.epsc[:, 0:1]),
                  reads=[stk], writes=[stk])
            P.add("dve", lambda e: e.reciprocal(out=st[:, 15:16], in_=st[:, 14:15]), reads=[stk], writes=[stk])
        else:
            P.add("act", lambda e: e.activation(out=st[:, 14:15], in_=st[:, 13:14], func=AF.Ln, bias=self.epsc[:, 0:1]),
                  reads=[stk], writes=[stk])
            P.add("act", lambda e: e.activation(out=st[:, 15:16], in_=st[:, 14:15], func=AF.Exp, scale=-0.5),
                  reads=[stk], writes=[stk])
        yield
        P.add("dve", lambda e: e.tensor_scalar(out=st[:, 16:17], in0=st[:, 12:13], scalar1=st[:, 15:16], scalar2=-1.0,
                                               op0=ALU.mult, op1=ALU.mult), reads=[stk], writes=[stk])
        if mode == "lnexp":
            P.add("dve", lambda e: e.tensor_scalar(out=out, in0=z, scalar1=st[:, 15:16], scalar2=st[:, 16:17], op0=ALU.mult, op1=ALU.add),
                  reads=[zk, stk], writes=[outk])
        else:
            P.add("act", lambda e: e.activation(out=out, in_=z, func=AF.Identity, scale=st[:, 15:16], bias=st[:, 16:17]),
                  reads=[zk, stk], writes=[outk])
        yield
        P.add(aff, lambda e: e.tensor_tensor(out=out, in0=out, in1=gbc, op=ALU.mult), reads=[outk, gk], writes=[outk])
        P.add(aff, lambda e: e.tensor_tensor(out=out, in0=out, in1=bbc, op=ALU.add), reads=[outk, bk], writes=[outk])

    def transposes_f32(self, src, srck, pp, bank0, dst_fn, dstk):
        P = self.P
        for c in range(8):
            half = c // 4
            outap = pp[:, half * 512 + (c % 4) * 128: half * 512 + (c % 4 + 1) * 128]
            P.add("pe", lambda e, outap=outap, c=c: e.transpose(out=outap, in_=src[:, c * 128:(c + 1) * 128], identity=self.idf),
                  reads=[srck], writes=[("ps", bank0 + half)])
        for half in range(2):
            src_ps = pp[:, half * 512:(half + 1) * 512].rearrange("p (a b) -> p a b", a=4)
            dst = dst_fn(half)
            if half == 0:
                P.add("act", lambda e, dst=dst, src_ps=src_ps: e.activation(out=dst, in_=src_ps, func=AF.Copy),
                      reads=[("ps", bank0 + half)], writes=[dstk])
            else:
                P.add("dve", lambda e, dst=dst, src_ps=src_ps: e.tensor_copy(out=dst, in_=src_ps),
                      reads=[("ps", bank0 + half)], writes=[dstk])

    def build(self):
        nc = bass.Bass("TRN2", target_bir_lowering=False)
        self.nc = nc
        nt = self.ntok
        dbg = self.debug
        inp = lambda name, shape, dt=F32: nc.dram_tensor(name, list(shape), dt, kind="ExternalInput").ap()
        scratch_kind = "ExternalOutput" if dbg else "Internal"
        scr = lambda name, shape, dt: nc.dram_tensor(name, list(shape), dt, kind=scratch_kind).ap()
        self.x = inp("x", [nt, D])
        self.w_in = inp("w_in", [128, 8, DP])
        self.w_o = inp("w_o", [128, 8, D])
        self.w_up = inp("w_up", [128, 8, 2 * DFF])
        self.w_dn = inp("w_dn", [128, NFC, D])
        self.ln_in_g = inp("ln_in_g", [D]); self.ln_in_b = inp("ln_in_b", [D])
        self.ln1_g = inp("ln1_g", [D]); self.ln1_b = inp("ln1_b", [D])
        self.ln2_g = inp("ln2_g", [D]); self.ln2_b = inp("ln2_b", [D])
        self.gn2 = inp("gn2", [512])
        self.smallp = inp("smallp", [128, 256])
        self.ident = inp("ident", [128, 128])
        self.rpbg = inp("rpbg", [128, 8, NBLK * 128])
        self.amask = inp("amask", [128, NBLK * 128])
        self.y = nc.dram_tensor("y", [nt, D], F32, kind="ExternalOutput").ap()
        self.d_xh = scr("d_xh", [nt, D], F32)
        self.d_qT = scr("d_qT", [4, 128, nt], BF16)
        self.d_kT = scr("d_kT", [4, 128, nt], BF16)
        self.d_va = scr("d_va", [nt, 520], BF16)
        self.d_bgT = scr("d_bgT", [4, 128, nt], F32)
        self.d_chT = scr("d_chT", [4, 128, nt], F32)
        self.d_x1 = scr("d_x1", [nt, D], F32)
        self.d_x1T = scr("d_x1T", [8, 128, nt], BF16)
        self.d_wup = nc.dram_tensor("d_wup", [128, 8 * 2 * DFF], BF16, kind="Internal").ap()
        self.d_wdn = nc.dram_tensor("d_wdn", [128, NFC * D], BF16, kind="Internal").ap()
        if dbg:
            self.d_yT = scr("d_yT", [8, 128, nt], BF16)

        with ExitStack() as st:
            self.A = Arena(nc, st, 207 * 1024 + 512)
            self.ps = st.enter_context(nc.psum_tensor("ps", [128, 4096], F32))
            self.pp = [self.ps[:, 1024 * i:1024 * (i + 1)] for i in range(4)]
            self.P = Prog(nc)
            self._chan = 0
            self.setup_consts()
            if "A" in self.stages:
                self.stage_A()
                self.P.barrier()
            if "B" in self.stages:
                self.stage_B()
                self.P.barrier()
            if "C" in self.stages:
                self.stage_C()
            self.P.emit(st)
        return nc

    def setup_consts(self):
        P, A = self.P, self.A
        self.idf = A.f32([128])
        self.idb = A.bf([128])
        self.nh = A.f32([512])
        self.sp_t = A.f32([256])
        self.onesf = A.f32([128])
        self.epsc = A.f32([16])
        self.st_rot = Rot("st", [A.f32([32]) for _ in range(6)])
        P.dma(("ch", self.chan()), self.idf, self.ident, writes=["idf"])
        P.dma(("ch", self.chan()), self.sp_t, self.smallp, writes=["smallp"])
        P.add("dve", lambda e: e.tensor_copy(out=self.idb, in_=self.idf), reads=["idf"], writes=["idb"])
        P.add("pool", lambda e: e.memset(self.nh, -0.5), writes=["nh"])
        P.add("pool", lambda e: e.memset(self.onesf, 1.0), writes=["onesf"])
        P.add("pool", lambda e: e.memset(self.epsc, EPS), writes=["epsc"])
        sp = self.sp_t
        self.cw = sp[:, 0:12].rearrange("p (c k) -> p c k", c=4)
        self.cb = sp[:, 12:16]
        self.gn1 = sp[:, 16:20]
        self.fw = sp[:, 20:152].rearrange("p (c k) -> p c k", c=44)
        self.fb = sp[:, 152:196]
        self.gn2c = sp[:, 196:200]
        self.persist_end = A.off
        self.P.barrier()

    def stage_A(self):
        P, A, pp = self.P, self.A, self.pp
        A.off = self.persist_end
        self._chan = 4
        win = A.bf([8, DP])
        gbc = A.f32([D]); bbc = A.f32([D])
        mark = A.off
        stg = [A.f32([DP]) for _ in range(2)]
        P.dma(("ch", self.chan()), gbc, self.ln_in_g.partition_broadcast(128), writes=["gbc"])
        P.dma(("ch", self.chan()), bbc, self.ln_in_b.partition_broadcast(128), writes=["bbc"])
        c0 = self._chan; self._chan += 2
        for kc in range(8):
            s = stg[kc % 2]; sk = ("wstg", kc % 2)
            P.dma(("ch", c0 + kc % 2), s, self.w_in[:, kc, :], writes=[sk])
            P.add("dve", lambda e, s=s, kc=kc: e.tensor_copy(out=win[:, kc, 0:1536], in_=s[:, 0:1536]), reads=[sk], writes=[("win", kc, 0)])
            P.add("act", lambda e, s=s, kc=kc: e.mul(out=win[:, kc, 1536:2048], in_=s[:, 1536:2048], mul=0.125), reads=[sk], writes=[("win", kc, 1)])
            P.add("pool", lambda e, s=s, kc=kc: e.tensor_copy(out=win[:, kc, 2048:3072], in_=s[:, 2048:3072]), reads=[sk], writes=[("win", kc, 2)])
        P.barrier()
        A.off = mark
        xin_rot = self.mk_rot("xin", [A.f32([D]) for _ in range(4)])
        xh_rot = self.mk_rot("xh", [A.f32([D]) for _ in range(2)])
        xT_rot = self.mk_rot("xT", [A.bf([8, 512]) for _ in range(2)], dma=False)
        e32 = self.mk_rot("e32", [A.f32([512]) for _ in range(6)])
        e16 = self.mk_rot("e16", [A.bf([512]) for _ in range(4)])
        cg_rot = self.mk_rot("cg", [A.f32([512]) for _ in range(2)], dma=False)
        va_rot = self.mk_rot("va", [A.bf([8, 65]) for _ in range(3)])
        for i, b in enumerate(va_rot.b):
            P.add("pool", lambda e, b=b: e.memset(b, 1.0), writes=[("va", i)])
        ntt = len(self.tt_info)
        xins = {}

        def load_x(g):
            b, k, ch = xin_rot.next()
            P.dma(ch, b, self.x[g * 128:(g + 1) * 128, :], writes=[k])
            xins[g] = (b, k)

        xTs = {}
        state = dict(bank_rr=0, evac_rr=0)

        def LN_T(m):
            xT, xTk, _ = xT_rot.next()
            xTs[m] = (xT, xTk)
            for j in range(4):
                g = 4 * m + j
                xin, xk = xins[g]
                xh, xhk, xhch = xh_rot.next()
                for _ in self.layernorm_gen(xin, xk, xh, xhk, gbc, bbc, "gbc", "bbc", mode="sqrt"):
                    yield
                P.dma(xhch, self.d_xh[g * 128:(g + 1) * 128, :], xh, reads=[xhk])
                yield
                yield
                ppt = pp[j % 2]
                self.transposes_f32(xh, xhk, ppt, 2 * (j % 2),
                                    lambda half, xT=xT, j=j: xT[:, 4 * half:4 * half + 4, 128 * j:128 * (j + 1)], xTk)
                yield

        def PROJ(m):
            t0 = 512 * m
            xT, xTk = xTs[m]
            order = [0, 1, 2, 3, 4, 8, 5, 9, 6, 10, 7, 11, 12, 13, 14, 15, 16, 17, 18, 19]
            cgbuf = None
            for oc in order:
                bank = 4 + state["bank_rr"] % 4
                state["bank_rr"] += 1
                pso = pp[bank // 2][:, (bank % 2) * 512:(bank % 2 + 1) * 512]
                for kc in range(8):
                    P.add("pe", lambda e, pso=pso, kc=kc, oc=oc: e.matmul(pso, lhsT=win[:, kc, oc * 128:(oc + 1) * 128], rhs=xT[:, kc, :],
                                                                  start=(kc == 0), stop=(kc == 7)),
                          reads=[xTk], writes=[("ps", bank)])
                if oc < 4:
                    b, k, ch = e32.next()
                    P.add("act", lambda e, b=b, pso=pso: e.activation(out=b, in_=pso, func=AF.Copy), reads=[("ps", bank)], writes=[k])
                    P.dma(ch, self.d_bgT[oc, :, t0:t0 + 512], b, reads=[k])
                elif oc < 8:
                    b, k, _ = cg_rot.next()
                    P.add("act", lambda e, b=b, pso=pso: e.activation(out=b, in_=pso, func=AF.Copy), reads=[("ps", bank)], writes=[k])
                    cgbuf = (b, k)
                elif oc < 12:
                    b, k, ch = e32.next()
                    cb_, ck_ = cgbuf
                    P.add("dve", lambda e, b=b, pso=pso, cb_=cb_: e.tensor_tensor(out=b, in0=pso, in1=cb_, op=ALU.mult),
                          reads=[("ps", bank), ck_], writes=[k])
                    P.dma(ch, self.d_chT[oc - 8, :, t0:t0 + 512], b, reads=[k])
                else:
                    b, k, ch = e16.next()
                    if state["evac_rr"] % 2 == 0:
                        P.add("act", lambda e, b=b, pso=pso: e.activation(out=b, in_=pso, func=AF.Copy), reads=[("ps", bank)], writes=[k])
                    else:
                        P.add("dve", lambda e, b=b, pso=pso: e.tensor_copy(out=b, in_=pso), reads=[("ps", bank)], writes=[k])
                    state["evac_rr"] += 1
                    dst = self.d_qT if oc < 16 else self.d_kT
                    P.dma(ch, dst[(oc - 12) % 4, :, t0:t0 + 512], b, reads=[k])
                yield
            for j in range(4):
                g = 4 * m + j
                bank = 4 + state["bank_rr"] % 4
                state["bank_rr"] += 1
                pso = pp[bank // 2][:, (bank % 2) * 512:(bank % 2 + 1) * 512]
                for kc in range(8):
                    P.add("pe", lambda e, pso=pso, kc=kc, j=j: e.matmul(pso, lhsT=xT[:, kc, 128 * j:128 * (j + 1)], rhs=win[:, kc, 2560:3072],
                                                                start=(kc == 0), stop=(kc == 7)),
                          reads=[xTk], writes=[("ps", bank)])
                b, k, ch = va_rot.next()
                src = pso.rearrange("p (h d) -> p h d", h=8)
                if state["evac_rr"] % 2 == 0:
                    P.add("act", lambda e, b=b, src=src: e.activation(out=b[:, :, 0:64], in_=src, func=AF.Copy), reads=[("ps", bank)], writes=[k])
                else:
                    P.add("dve", lambda e, b=b, src=src: e.tensor_copy(out=b[:, :, 0:64], in_=src), reads=[("ps", bank)], writes=[k])
                state["evac_rr"] += 1
                P.dma(ch, self.d_va[g * 128:(g + 1) * 128, :], b.rearrange("p h d -> p (h d)"), reads=[k])
                yield

        PIECE = 1408
        wc_in = self.mk_rot("wcin", [A.f32([PIECE]) for _ in range(2)])
        wc_out = self.mk_rot("wcout", [A.bf([PIECE]) for _ in range(2)])

        def WC():
            srcs = [(self.w_up.rearrange("p a b -> p (a b)"), self.d_wup, 8 * 2 * DFF),
                    (self.w_dn.rearrange("p a b -> p (a b)"), self.d_wdn, NFC * D)]
            n = 0
            for src, dst, tot in srcs:
                for o in range(0, tot, PIECE):
                    bi, ki, chi = wc_in.next()
                    bo, ko, cho = wc_out.next()
                    P.dma(chi, bi, src[:, o:o + PIECE], writes=[ki])
                    for _ in range(5):
                        yield
                    if n % 2 == 0:
                        P.add("act", lambda e, bi=bi, bo=bo: e.activation(out=bo, in_=bi, func=AF.Copy), reads=[ki], writes=[ko])
                    else:
                        P.add("dve", lambda e, bi=bi, bo=bo: e.tensor_copy(out=bo, in_=bi), reads=[ki], writes=[ko])
                    n += 1
                    for _ in range(5):
                        yield
                    P.dma(cho, dst[:, o:o + PIECE], bo, reads=[ko])
                    yield

        for j in range(4):
            load_x(j)
        for _ in LN_T(0):
            pass
        gw = WC() if "C" in self.stages else None
        for m in range(self.nmacro):
            if m + 1 < self.nmacro:
                for j in range(4):
                    load_x(4 * (m + 1) + j)
            ga = PROJ(m)
            gb = LN_T(m + 1) if m + 1 < self.nmacro else None
            while ga is not None or gb is not None:
                if ga is not None and next(ga, "end") == "end":
                    ga = None
                if gb is not None and next(gb, "end") == "end":
                    gb = None
                if gw is not None and next(gw, "end") == "end":
                    gw = None
        if gw is not None:
            for _ in gw:
                pass

    def stage_B(self):
        P, A, pp, ps = self.P, self.A, self.pp, self.ps
        A.off = self.persist_end
        self._chan = 4
        wo = A.bf([8, D])
        bias = A.bf([8 * NBLK, 128])
        g1bc = A.f32([D]); b1bc = A.f32([D]); gn2bc = A.f32([512])
        mark = A.off
        P.dma(("ch", self.chan()), g1bc, self.ln1_g.partition_broadcast(128), writes=["g1bc"])
        P.dma(("ch", self.chan()), b1bc, self.ln1_b.partition_broadcast(128), writes=["b1bc"])
        P.dma(("ch", self.chan()), gn2bc, self.gn2.partition_broadcast(128), writes=["gn2bc"])
        stg = [A.f32([4 * D]) for _ in range(2)]
        c0 = self._chan; self._chan += 2
        for hlf in range(2):
            s = stg[hlf]; sk = ("wstg", hlf)
            P.dma(("ch", c0 + hlf), s, self.w_o[:, 4 * hlf:4 * hlf + 4, :].rearrange("p a b -> p (a b)"), writes=[sk])
            if hlf == 0:
                P.add("dve", lambda e, s=s, hlf=hlf: e.tensor_copy(out=wo[:, 0:4, :].rearrange("p a b -> p (a b)"), in_=s), reads=[sk], writes=[("wo", hlf)])
            else:
                for c in range(4):
                    P.add("dve", lambda e, s=s, c=c: e.tensor_scalar(out=wo[:, 4 + c, :], in0=s[:, c * D:(c + 1) * D], scalar1=self.gn2c[:, c:c + 1], scalar2=None, op0=ALU.mult),
                          reads=[sk], writes=[("wo", 1)])
        msk = A.f32([NBLK * 128])
        bst = [A.f32([NBLK * 128]) for _ in range(2)]
        P.dma(("ch", self.chan()), msk, self.amask, writes=["msk"])
        c0 = self._chan; self._chan += 2
        for h in range(8):
            s = bst[h % 2]; sk = ("bst", h % 2)
            P.dma(("ch", c0 + h % 2), s, self.rpbg[:, h, :], writes=[sk])
            eng = "dve" if h % 2 == 0 else "pool"
            P.add(eng, lambda e, s=s, h=h: e.tensor_tensor(out=bias[:, h * NBLK:(h + 1) * NBLK, :].rearrange("p a b -> p (a b)"), in0=s, in1=msk, op=ALU.add),
                  reads=[sk, "msk"], writes=[("bias", h)])
        P.barrier()
        A.off = mark
        qm_rot = self.mk_rot("qm", [A.bf([4, 512]) for _ in range(2)])
        kw_rot = self.mk_rot("kw", [A.bf([4, 640]) for _ in range(2)])
        vw_rot = self.mk_rot("vw", [A.bf([5, 520]) for _ in range(2)])
        sb_rot = self.mk_rot("sb", [A.f32([640]) for _ in range(3)], dma=False)
        pt_rot = self.mk_rot("pt", [A.bf([640]) for _ in range(4)], dma=False)
        ya_rot = self.mk_rot("ya", [A.f32([512]) for _ in range(2)], dma=False)
        yan_rot = self.mk_rot("yan", [A.bf([512]) for _ in range(2)], dma=False)
        yT_rot = self.mk_rot("yT", [A.bf([8, 512]) for _ in range(2)], dma=False)
        chw_rot = self.mk_rot("chw", [A.f32([4, 516]) for _ in range(1)])
        bgw_rot = self.mk_rot("bgw", [A.f32([4, 512]) for _ in range(1)])
        ab_rot = self.mk_rot("ab", [A.f32([512]) for _ in range(2)], dma=False)
        sq4 = A.f32([4, 512])
        sq_rot = self.mk_rot("sq", [A.f32([512])], dma=False)
        rbc = A.f32([512])
        rd_rot = self.mk_rot("rd", [A.f32([16]) for _ in range(3)], dma=False)
        xh_rot = self.mk_rot("xhB", [A.f32([D]) for _ in range(3)])
        x1_rot = self.mk_rot("x1", [A.f32([D]) for _ in range(2)])
        x1T_rot = self.mk_rot("x1T", [A.bf([8, 512]) for _ in range(2)])
        kTv = self.d_kT.rearrange("c p t -> p c t")
        qTv = self.d_qT.rearrange("c p t -> p c t")
        bgTv = self.d_bgT.rearrange("c p t -> p c t")
        chTv = self.d_chT.rearrange("c p t -> p c t")
        x1Tv = self.d_x1T.rearrange("c p t -> p c t")
        ntt = len(self.tt_info)
        mac = {}
        tts = {}

        mloads = {}

        def macro_loads(m):
            t0 = 512 * m
            sbase, R, _ = self.tt_info[4 * m]
            send = sbase + R * 64
            at_start = (t0 == sbase)
            at_end = (t0 + 512 == send)
            qm, qmk, qch = qm_rot.next()
            P.dma(qch, qm, qTv[:, :, t0:t0 + 512], writes=[qmk])
            chw, chwk, chch = chw_rot.next()
            bgw, bgwk, bgch = bgw_rot.next()
            lo_c = 1 if at_start else 0
            hi_c = 513 if at_end else 514
            if at_start:
                P.add("pool", lambda e, chw=chw: e.memset(chw[:, :, 0:1], 0.0), writes=[chwk])
            if at_end:
                P.add("pool", lambda e, chw=chw: e.memset(chw[:, :, 513:514], 0.0), writes=[chwk])
            P.dma(chch, chw[:, :, lo_c:hi_c], chTv[:, :, t0 - 1 + lo_c:t0 - 1 + hi_c], writes=[chwk])
            P.dma(bgch, bgw, bgTv[:, :, t0:t0 + 512], writes=[bgwk])
            mloads[m] = (qm, qmk, chw, chwk, bgw, bgwk)

        def macro_begin(m):
            t0 = 512 * m
            qm, qmk, chw, chwk, bgw, bgwk = mloads[m]
            yT, yTk, _ = yT_rot.next()
            x1T, x1Tk, x1Tch = x1T_rot.next()
            mac[m] = dict(qm=qm, qmk=qmk, yT=yT, yTk=yTk, x1T=x1T, x1Tk=x1Tk, x1Tch=x1Tch, t0=t0)

        def CM(m):
            qm, qmk, chw, chwk, bgw, bgwk = mloads[m]
            yT, yTk = mac[m]["yT"], mac[m]["yTk"]
            psum_sum = ps[:, 3584:4096]
            sqs = []
            for c in range(4):
                ab, abk, _ = ab_rot.next()
                P.add("dve", lambda e, ab=ab, c=c, chw=chw: e.tensor_scalar(out=ab, in0=chw[:, c, 1:513], scalar1=self.cw[:, c, 1:2], scalar2=self.cb[:, c:c + 1],
                                                                    op0=ALU.mult, op1=ALU.add), reads=[chwk], writes=[abk])
                P.add("dve", lambda e, ab=ab, c=c, chw=chw: e.scalar_tensor_tensor(out=ab, in0=chw[:, c, 0:512], scalar=self.cw[:, c, 0:1], in1=ab,
                                                                           op0=ALU.mult, op1=ALU.add), reads=[chwk, abk], writes=[abk])
                P.add("dve", lambda e, ab=ab, c=c, chw=chw: e.scalar_tensor_tensor(out=ab, in0=chw[:, c, 2:514], scalar=self.cw[:, c, 2:3], in1=ab,
                                                                           op0=ALU.mult, op1=ALU.add), reads=[chwk, abk], writes=[abk])
                yc_buf = bgw
                P.add("pool", lambda e, ab=ab, c=c, bgw=bgw: e.tensor_tensor(out=bgw[:, c, :], in0=ab, in1=bgw[:, c, :], op=ALU.mult),
                      reads=[abk, bgwk], writes=[("yc", c), bgwk])
                sq, sqk = sq4[:, c, :], ("sq4", c)
                P.add("act", lambda e, sq=sq, c=c, yc_buf=yc_buf: e.activation(out=sq, in_=yc_buf[:, c, :], func=AF.Square), reads=[("yc", c)], writes=[sqk])
                yield
                sqs.append((sq, sqk, c))
            for sq, sqk, c in sqs:
                P.add("pe", lambda e, sq=sq, c=c: e.matmul(psum_sum, lhsT=self.onesf, rhs=sq, start=(c == 0), stop=(c == 3)),
                      reads=[sqk, "onesf"], writes=[("ps", 7)])
            P.add("act", lambda e: e.activation(out=rbc, in_=psum_sum, func=AF.Ln, scale=1.0 / 512, bias=self.epsc[:, 0:1]),
                  reads=[("ps", 7)], writes=["rbc"])
            P.add("act", lambda e: e.activation(out=rbc, in_=rbc, func=AF.Exp, scale=-0.5), reads=["rbc"], writes=["rbc"])
            yield
            for c in range(4):
                P.add("dve", lambda e, c=c, yT=yT, yc_buf=yc_buf: e.scalar_tensor_tensor(out=yT[:, c, :], in0=yc_buf[:, c, :], scalar=self.gn1[:, c:c + 1], in1=rbc,
                                                                   op0=ALU.mult, op1=ALU.mult), reads=[("yc", c), "rbc", bgwk], writes=[yTk])
                yield

        loads = {}

        def tt_loads(g):
            sb_, R, r0 = self.tt_info[g]
            var, lo, nch = tt_geom(R, r0)
            nk = 128 * nch
            ktok = sb_ + 64 * lo
            kw, kwk, kch = kw_rot.next()
            vw, vwk, vch = vw_rot.next()
            P.dma(kch, kw[:, :, 0:nk], kTv[:, :, ktok:ktok + nk], writes=[kwk])
            P.dma(vch, vw[:, 0:nch, :], self.d_va[ktok:ktok + nk, :].rearrange("(c p) d -> p c d", p=128), writes=[vwk])
            xh, xhk, xhch = xh_rot.next()
            P.dma(xhch, xh, self.d_xh[g * 128:(g + 1) * 128, :], writes=[xhk])
            loads[g] = (kw, kwk, vw, vwk, xh, xhk)

        def P1(g):
            m, j = g // 4, g % 4
            M = mac[m]
            qm, qmk = M["qm"], M["qmk"]
            sb_, R, r0 = self.tt_info[g]
            var, lo, nch = tt_geom(R, r0)
            nk = 128 * nch
            ktok = sb_ + 64 * lo
            kw, kwk, vw, vwk, xh, xhk = loads[g]
            if g + 1 < ntt:
                tt_loads(g + 1)
            pts = [None] * 8

            def geom(h):
                if h % 2 == 0:
                    start = 0
                    pos = lambda c: c
                    keys = [("ps", 0)] + ([("ps", 1)] if nch == 5 else [])
                    slot0 = VAR_BASE[var]
                else:
                    if nch == 5:
                        start = 896
                        pos = lambda c: (c + 1) if c < 4 else 0
                        slot0 = 0
                    else:
                        start = 1024
                        pos = lambda c: c
                        slot0 = VAR_BASE[var]
                    keys = [("ps", 2)] + ([("ps", 1)] if nch == 5 else [])
                return start, pos, keys, slot0

            def emit_S(h):
                p_, hf = h // 2, h % 2
                start, pos, keys, slot0 = geom(h)
                for c in range(nch):
                    col = start + 128 * pos(c)
                    bk_ = col // 512
                    P.add("pe", lambda e, col=col, c=c, p_=p_, hf=hf: e.matmul(ps[:, col:col + 128], lhsT=kw[64 * hf:64 * hf + 64, p_, c * 128:(c + 1) * 128],
                                                                        rhs=qm[64 * hf:64 * hf + 64, p_, 128 * j:128 * (j + 1)], start=True, stop=not BIAS_ON_PE),
                          reads=[kwk, qmk], writes=[("ps", bk_)])
                    if BIAS_ON_PE:
                        blk = h * NBLK + VAR_BASE[var] + c
                        P.add("pe", lambda e, col=col, blk=blk: e.matmul(ps[:, col:col + 128], lhsT=self.idb, rhs=bias[:, blk, :], start=False, stop=True),
                              reads=[], writes=[("ps", bk_)])
                pt, ptk, _ = pt_rot.next()
                if BIAS_ON_PE:
                    P.add("act", lambda e, pt=pt, start=start: e.activation(out=pt[:, 0:nk], in_=ps[:, start:start + nk], func=AF.Exp), reads=keys, writes=[ptk])
                else:
                    sbf, sbk, _ = sb_rot.next()
                    blk0 = h * NBLK + slot0
                    P.add("dve", lambda e, sbf=sbf, start=start, blk0=blk0: e.tensor_tensor(out=sbf[:, 0:nk], in0=ps[:, start:start + nk],
                                                                                   in1=bias[:, blk0:blk0 + nch, :].rearrange("p a b -> p (a b)"), op=ALU.add),
                          reads=keys, writes=[sbk])
                    P.add("act", lambda e, pt=pt, sbf=sbf: e.activation(out=pt[:, 0:nk], in_=sbf[:, 0:nk], func=AF.Exp), reads=[sbk], writes=[ptk])
                pts[h] = (pt, ptk, pos)

            def emit_PV(h):
                pt, ptk, pos = pts[h]
                if h < 7:
                    ocol, ob = 1536 + 65 * h, 3
                else:
                    ocol, ob = 2048, 4
                for c in range(nch):
                    pc = 128 * pos(c)
                    P.add("pe", lambda e, c=c, pc=pc, pt=pt, h=h, ocol=ocol: e.matmul(ps[:, ocol:ocol + 65], lhsT=pt[:, pc:pc + 128], rhs=vw[:, c, h * 65:(h + 1) * 65],
                                                                              start=(c == 0), stop=(c == nch - 1)),
                          reads=[ptk, vwk], writes=[("ps", ob)])
            for h in range(S_LOOKAHEAD):
                emit_S(h)
            yield
            for h in range(8):
                if h + S_LOOKAHEAD < 8:
                    emit_S(h + S_LOOKAHEAD)
                emit_PV(h)
                yield
            rd, rdk, _ = rd_rot.next()
            ya, yak, _ = ya_rot.next()
            Ob = ps[:, 1536:1536 + 455].rearrange("p (h d) -> p h d", h=7)
            P.add("dve", lambda e: e.reciprocal(out=rd[:, 0:7], in_=Ob[:, :, 64]), reads=[("ps", 3)], writes=[rdk])
            P.add("dve", lambda e: e.reciprocal(out=rd[:, 7:8], in_=ps[:, 2112:2113]), reads=[("ps", 4)], writes=[rdk])
            P.add("dve", lambda e: e.tensor_tensor(out=ya[:, 0:448].rearrange("p (h d) -> p h d", h=7), in0=Ob[:, :, 0:64],
                                                   in1=rd[:, 0:7].unsqueeze(2).broadcast_to([128, 7, 64]), op=ALU.mult),
                  reads=[("ps", 3), rdk], writes=[yak])
            P.add("dve", lambda e: e.tensor_scalar(out=ya[:, 448:512], in0=ps[:, 2048:2112], scalar1=rd[:, 7:8], scalar2=None, op0=ALU.mult),
                  reads=[("ps", 4), rdk], writes=[yak])
            sq, sqk, _ = sq_rot.next()
            P.add("act", lambda e: e.activation(out=sq, in_=ya, func=AF.Square, accum_out=rd[:, 8:9]), reads=[yak], writes=[sqk, rdk])
            P.add("act", lambda e: e.activation(out=rd[:, 9:10], in_=rd[:, 8:9], func=AF.Ln, scale=1.0 / 512, bias=self.epsc[:, 0:1]),
                  reads=[rdk], writes=[rdk])
            P.add("act", lambda e: e.activation(out=rd[:, 10:11], in_=rd[:, 9:10], func=AF.Exp, scale=-0.5), reads=[rdk], writes=[rdk])
            yan, yank, _ = yan_rot.next()
            P.add("act", lambda e: e.activation(out=yan, in_=ya, func=AF.Copy, scale=rd[:, 10:11]), reads=[yak, rdk], writes=[yank])
            tts[g] = dict(m=m, j=j, xh=xh, xhk=xhk, yan=yan, yank=yank)
            yield

        def P2(g):
            T = tts[g]
            M = mac[T["m"]]
            j, xh, xhk, yan, yank = T["j"], T["xh"], T["xhk"], T["yan"], T["yank"]
            yT, yTk = M["yT"], M["yTk"]
            tpb = ps[:, 2048 + 128:2048 + 384].bitcast(BF16)
            for c in range(4):
                P.add("pe", lambda e, c=c: e.transpose(out=tpb[:, c * 128:(c + 1) * 128], in_=yan[:, c * 128:(c + 1) * 128], identity=self.idb),
                      reads=[yank], writes=[("ps", 4)])
            P.add("act", lambda e: e.activation(out=yT[:, 4:8, 128 * j:128 * (j + 1)], in_=tpb.rearrange("p (a b) -> p a b", a=4), func=AF.Copy),
                  reads=[("ps", 4)], writes=[yTk])
            yield
            for hlf in range(2):
                W = ps[:, 2560 + 512 * hlf:3072 + 512 * hlf]
                for kc in range(8):
                    P.add("pe", lambda e, hlf=hlf, kc=kc, W=W: e.matmul(W, lhsT=yT[:, kc, 128 * j:128 * (j + 1)],
                                                                  rhs=wo[:, kc, hlf * 512:(hlf + 1) * 512], start=(kc == 0), stop=(kc == 7)),
                          reads=[yTk], writes=[("ps", 5 + hlf)])
                yield
            P.add("dve", lambda e: e.scalar_tensor_tensor(out=xh, in0=xh, scalar=ALPHA, in1=ps[:, 2560:3584], op0=ALU.mult, op1=ALU.add),
                  reads=[xhk, ("ps", 5), ("ps", 6)], writes=[xhk])
            yield
            x1, x1k, x1ch = x1_rot.next()
            for _ in self.layernorm_gen(xh, xhk, x1, x1k, g1bc, b1bc, "g1bc", "b1bc", mode="lnexp", aff=LN1_AFF):
                yield
            P.dma(x1ch, self.d_x1[g * 128:(g + 1) * 128, :], x1, reads=[x1k])
            T["x1"], T["x1k"] = x1, x1k
            yield

        def P3(g):
            T = tts[g]
            M = mac[T["m"]]
            j, x1, x1k = T["j"], T["x1"], T["x1k"]
            x1T, x1Tk = M["x1T"], M["x1Tk"]
            tp = ps[:, 3584:4096]
            for r in range(2):
                for c in range(4):
                    cc = 4 * r + c
                    P.add("pe", lambda e, c=c, cc=cc: e.transpose(out=tp[:, c * 128:(c + 1) * 128], in_=x1[:, cc * 128:(cc + 1) * 128], identity=self.idf),
                          reads=[x1k], writes=[("ps", 7)])
                dst = x1T[:, 4 * r:4 * r + 4, 128 * j:128 * (j + 1)]
                src = tp.rearrange("p (a b) -> p a b", a=4)
                P.add("act", lambda e, dst=dst, src=src: e.activation(out=dst, in_=src, func=AF.Copy), reads=[("ps", 7)], writes=[x1Tk])
                yield
            if j == 3:
                P.dma(M["x1Tch"], x1Tv[:, :, M["t0"]:M["t0"] + 512], x1T, reads=[x1Tk])
                if self.debug:
                    P.dma(("ch", 90 + T["m"] % 2), self.d_yT.rearrange("c p t -> p c t")[:, :, M["t0"]:M["t0"] + 512], M["yT"], reads=[M["yTk"]])

        macro_loads(0)
        macro_begin(0)
        for _ in CM(0):
            pass
        if self.nmacro > 1:
            macro_loads(1)
        tt_loads(0)
        g0 = None
        for g in range(ntt + 2):
            g1 = P1(g) if g < ntt else None
            g2 = P2(g - 1) if 0 <= g - 1 < ntt else None
            g3 = P3(g - 2) if 0 <= g - 2 < ntt else None
            if g % 4 == 1 and g // 4 + 1 < self.nmacro:
                macro_begin(g // 4 + 1)
                g0 = CM(g // 4 + 1)
            step = 0
            while g1 is not None or g2 is not None or g3 is not None:
                if g1 is not None and next(g1, "end") == "end":
                    g1 = None
                if g3 is not None and step >= 1 and next(g3, "end") == "end":
                    g3 = None
                if g2 is not None and (step >= P2_START or g1 is None) and next(g2, "end") == "end":
                    g2 = None
                if g0 is not None and step % 2 == 1 and next(g0, "end") == "end":
                    g0 = None
                step += 1
            if g % 4 == 3:
                if g0 is not None:
                    for _ in g0:
                        pass
                    g0 = None
                if g // 4 + 2 < self.nmacro:
                    macro_loads(g // 4 + 2)

    def stage_C(self):
        P, A, pp = self.P, self.A, self.pp
        A.off = self.persist_end
        self._chan = 4
        wup = A.bf([8, 2 * DFF])
        wdn = A.bf([NFC, D])
        g2bc = A.f32([D]); b2bc = A.f32([D])
        mark = A.off
        P.dma(("ch", self.chan()), g2bc, self.ln2_g.partition_broadcast(128), writes=["g2bc"])
        P.dma(("ch", self.chan()), b2bc, self.ln2_b.partition_broadcast(128), writes=["b2bc"])
        c0 = self._chan; self._chan += 2
        for kc in range(8):
            P.dma(("ch", c0 + kc % 2), wup[:, kc, :], self.d_wup[:, kc * 2 * DFF:(kc + 1) * 2 * DFF], writes=[("wup", kc)])
        c0 = self._chan; self._chan += 2
        for q in range(2):
            P.dma(("ch", c0 + q), wdn[:, 11 * q:11 * q + 11, :].rearrange("p a b -> p (a b)"), self.d_wdn[:, 11 * q * D:(11 * q + 11) * D], writes=[("wdn", q)])
        P.barrier()
        A.off = mark
        xw_rot = self.mk_rot("xw", [A.bf([8, 516]) for _ in range(1)])
        hT = A.bf([NFC, 512])
        halo_sb = A.f32([88])
        U_rot = self.mk_rot("U", [A.f32([516]) for _ in range(3)], dma=False)
        ab_rot = self.mk_rot("abC", [A.f32([512]) for _ in range(6)], dma=False)
        x1_rot = self.mk_rot("x1C", [A.f32([D]) for _ in range(3)])
        x1Tv = self.d_x1T.rearrange("c p t -> p c t")
        gelu = getattr(AF, GELU_FUNC)
        bank_rr = 0
        xws = {}
        x1s = {}

        def load_xw(m):
            t0 = 512 * m
            sbase, R, _ = self.tt_info[4 * m]
            send = sbase + R * 64
            at_start = (t0 == sbase)
            at_end = (t0 + 512 == send)
            xw, xwk, xwch = xw_rot.next()
            lo_c = 1 if at_start else 0
            hi_c = 513 if at_end else 514
            if at_start:
                P.add("pool", lambda e, xw=xw: e.memset(xw[:, :, 0:1], 0.0), writes=[xwk])
            if at_end:
                P.add("pool", lambda e, xw=xw: e.memset(xw[:, :, 513:514], 0.0), writes=[xwk])
            P.dma(xwch, xw[:, :, lo_c:hi_c], x1Tv[:, :, t0 - 1 + lo_c:t0 - 1 + hi_c], writes=[xwk])
            xws[m] = (xw, xwk)

        x1chs = {}

        def load_x1(g):
            x1, x1k, x1ch = x1_rot.next()
            P.dma(x1ch, x1, self.d_x1[g * 128:(g + 1) * 128, :], writes=[x1k])
            x1s[g] = (x1, x1k)
            x1chs[g] = x1ch

        for m in range(self.nmacro):
            t0 = 512 * m
            if m == 0:
                load_xw(0)
            xw, xwk = xws[m]
            hps = pp[2][:, 0:88]
            for cc in range(44):
                for kc in range(8):
                    P.add("pe", lambda e, cc=cc, kc=kc, xw=xw: e.matmul(hps[:, 2 * cc:2 * cc + 2], lhsT=wup[:, kc, cc * 128:(cc + 1) * 128], rhs=xw[:, kc, 0:514:513],
                                                                start=(kc == 0), stop=(kc == 7)),
                          reads=[xwk], writes=[("ps", 4)])
            P.add("act", lambda e: e.activation(out=halo_sb, in_=hps, func=AF.Copy), reads=[("ps", 4)], writes=["halo"])
            pend_gelu = []
            pend_q = []
            for f in range(NFC):
                info = []
                for which in range(2):
                    cc = f + which * NFC
                    bank = bank_rr % 4
                    bank_rr += 1
                    pso = pp[bank // 2][:, (bank % 2) * 512:(bank % 2 + 1) * 512]
                    for kc in range(8):
                        P.add("pe", lambda e, pso=pso, cc=cc, kc=kc, xw=xw: e.matmul(pso, lhsT=wup[:, kc, cc * 128:(cc + 1) * 128], rhs=xw[:, kc, 1:513],
                                                                             start=(kc == 0), stop=(kc == 7)),
                              reads=[xwk], writes=[("ps", bank)])
                    U, Uk, _ = U_rot.next()
                    ab, abk, _ = ab_rot.next()
                    P.add("act", lambda e, U=U, pso=pso: e.activation(out=U[:, 1:513], in_=pso, func=AF.Copy), reads=[("ps", bank)], writes=[Uk])
                    P.add("act", lambda e, ab=ab, pso=pso, cc=cc: e.activation(out=ab, in_=pso, func=AF.Identity, scale=self.fw[:, cc, 1:2], bias=self.fb[:, cc:cc + 1]),
                          reads=[("ps", bank)], writes=[abk])
                    info.append((U, Uk, ab, abk, cc))
                if pend_gelu:
                    ag, agk, av, avk, fp = pend_gelu.pop(0)
                    P.add("act", lambda e, ag=ag: e.activation(out=ag, in_=ag, func=gelu), reads=[agk], writes=[agk])
                    pend_q.append((ag, agk, av, avk, fp))
                for (U, Uk, ab, abk, cc) in info:
                    P.add("dve", lambda e, U=U, cc=cc: e.tensor_copy(out=U[:, 0:514:513], in_=halo_sb[:, 2 * cc:2 * cc + 2]), reads=["halo"], writes=[Uk])
                for (U, Uk, ab, abk, cc) in info:
                    P.add("dve", lambda e, ab=ab, U=U, cc=cc: e.scalar_tensor_tensor(out=ab, in0=U[:, 0:512], scalar=self.fw[:, cc, 0:1], in1=ab,
                                                                             op0=ALU.mult, op1=ALU.add), reads=[Uk, abk], writes=[abk])
                if len(pend_q) > 1 or (pend_q and not pend_gelu and False):
                    ag, agk, av, avk, fp = pend_q.pop(0)
                    P.add("dve", lambda e, ag=ag, av=av, fp=fp: e.tensor_tensor(out=hT[:, fp, :], in0=ag, in1=av, op=ALU.mult), reads=[agk, avk], writes=[("hT", fp)])
                for (U, Uk, ab, abk, cc) in info:
                    P.add("dve", lambda e, ab=ab, U=U, cc=cc: e.scalar_tensor_tensor(out=ab, in0=U[:, 2:514], scalar=self.fw[:, cc, 2:3], in1=ab,
                                                                             op0=ALU.mult, op1=ALU.add), reads=[Uk, abk], writes=[abk])
                (_, _, ag, agk, _), (_, _, av, avk, _) = info
                pend_gelu.append((ag, agk, av, avk, f))
            while pend_gelu:
                ag, agk, av, avk, fp = pend_gelu.pop(0)
                P.add("act", lambda e, ag=ag: e.activation(out=ag, in_=ag, func=gelu), reads=[agk], writes=[agk])
                pend_q.append((ag, agk, av, avk, fp))
            while pend_q:
                ag, agk, av, avk, fp = pend_q.pop(0)
                P.add("dve", lambda e, ag=ag, av=av, fp=fp: e.tensor_tensor(out=hT[:, fp, :], in0=ag, in1=av, op=ALU.mult), reads=[agk, avk], writes=[("hT", fp)])
            if m + 1 < self.nmacro:
                load_xw(m + 1)
            load_x1(4 * m)
            load_x1(4 * m + 1)
            for j in range(4):
                g = 4 * m + j
                x1, x1k = x1s[g]
                banks = (5, 6) if j % 2 == 0 else (7, 4)
                outs = [pp[b // 2][:, (b % 2) * 512:(b % 2 + 1) * 512] for b in banks]
                for hlf in range(2):
                    for f in range(NFC):
                        P.add("pe", lambda e, hlf=hlf, f=f, j=j, o=outs[hlf]: e.matmul(o, lhsT=hT[:, f, 128 * j:128 * (j + 1)], rhs=wdn[:, f, hlf * 512:(hlf + 1) * 512],
                                                                           start=(f == 0), stop=(f == NFC - 1)),
                              reads=[("hT", f)], writes=[("ps", banks[hlf])])
                for hlf in range(2):
                    P.add("dve", lambda e, hlf=hlf, x1=x1, o=outs[hlf]: e.scalar_tensor_tensor(out=x1[:, hlf * 512:(hlf + 1) * 512], in0=x1[:, hlf * 512:(hlf + 1) * 512], scalar=ALPHA,
                                                                                  in1=o, op0=ALU.mult, op1=ALU.add),
                          reads=[x1k, ("ps", banks[hlf])], writes=[x1k])
                if j + 2 < 4:
                    load_x1(g + 2)
                self.layernorm(x1, x1k, x1, x1k, g2bc, b2bc, "g2bc", "b2bc")
                P.dma(x1chs[g], self.y[g * 128:(g + 1) * 128, :], x1, reads=[x1k], final=True)


_NC_CACHE = {}


def prep_shared(inputs):
    f = lambda a: np.ascontiguousarray(np.asarray(a, dtype=np.float32))
    w_in = f(inputs["w_in"])[0].reshape(8, 128, DP).transpose(1, 0, 2)
    w_o = f(inputs["w_o"])[0].reshape(8, 128, D).transpose(1, 0, 2)
    w_up = f(inputs["w_up"])[0].reshape(8, 128, 2 * DFF).transpose(1, 0, 2)
    w_dn = f(inputs["w_down"])[0].reshape(NFC, 128, D).transpose(1, 0, 2)
    smallp = np.zeros((128, 256), np.float32)
    cw = f(inputs["conv_w"])[0]
    smallp[:, 0:12] = cw.reshape(3, 4, 128).transpose(2, 1, 0).reshape(128, 12)
    smallp[:, 12:16] = f(inputs["conv_b"])[0].reshape(4, 128).T
    gn = f(inputs["gn_g"])[0]
    smallp[:, 16:20] = gn[:512].reshape(4, 128).T
    fw = f(inputs["ffn_conv_w"])[0]
    smallp[:, 20:152] = fw.reshape(3, 44, 128).transpose(2, 1, 0).reshape(128, 132)
    smallp[:, 152:196] = f(inputs["ffn_conv_b"])[0].reshape(44, 128).T
    smallp[:, 196:200] = gn[512:].reshape(4, 128).T
    dr_idx, dc_idx, valid = bias_index_tables()
    rpb = f(inputs["rpb"])[0]
    g_ = rpb[:, dr_idx, dc_idx]
    rpbg = np.ascontiguousarray(g_.transpose(2, 0, 1, 3).reshape(128, 8, NBLK * 128))
    amask = np.ascontiguousarray(np.where(valid, np.float32(0.0), np.float32(NEG)).transpose(1, 0, 2).reshape(128, NBLK * 128)).astype(np.float32)
    sh = {
        "w_in": np.ascontiguousarray(w_in), "w_o": np.ascontiguousarray(w_o),
        "w_up": np.ascontiguousarray(w_up), "w_dn": np.ascontiguousarray(w_dn),
        "ln_in_g": f(inputs["ln_in_g"]), "ln_in_b": f(inputs["ln_in_b"]),
        "ln1_g": f(inputs["ln1_g"])[0], "ln1_b": f(inputs["ln1_b"])[0],
        "ln2_g": f(inputs["ln2_g"])[0], "ln2_b": f(inputs["ln2_b"])[0],
        "gn2": np.ascontiguousarray(gn[512:]),
        "smallp": smallp, "ident": np.eye(128, dtype=np.float32),
        "rpbg": rpbg, "amask": amask,
    }
    return sh


def kernel(**inputs):
    xp = np.asarray(inputs["x_prompt"], dtype=np.float32)
    xs = np.asarray(inputs["x_sample"], dtype=np.float32)
    n = 8
    bp, sp_, _ = xp.shape
    bs, ss_, _ = xs.shape
    pp_ = bp // n
    ps_ = bs // n
    seq_lens = [sp_] * pp_ + [ss_] * ps_
    key = tuple(seq_lens)
    if key not in _NC_CACHE:
        _NC_CACHE[key] = Builder(seq_lens).build()
    nc = _NC_CACHE[key]
    sh = prep_shared(inputs)
    in_maps = []
    for c in range(n):
        xc = np.concatenate([xp[c * pp_:(c + 1) * pp_].reshape(-1, D), xs[c * ps_:(c + 1) * ps_].reshape(-1, D)], axis=0)
        d = dict(sh)
        d["x"] = np.ascontiguousarray(xc)
        in_maps.append(d)
    res = run_bass_kernel_spmd(nc, in_maps, core_ids=list(range(n)))
    yp = np.empty_like(xp)
    ys = np.empty_like(xs)
    for c in range(n):
        yc = np.asarray(res.results[c]["y"], dtype=np.float32)
        yp[c * pp_:(c + 1) * pp_] = yc[:pp_ * sp_].reshape(pp_, sp_, D)
        ys[c * ps_:(c + 1) * ps_] = yc[pp_ * sp_:].reshape(ps_, ss_, D)
    return (yp, ys)
```
